# Optimizing a Trainium2 kernel written in Bass

```python
import jax
import jax.numpy as jnp
from jax import lax
import numpy as np

D_MODEL = 1024
BATCH = 8
SEQ = 2048
DEPTH = 4

CHUNK = 64
EPS = 1e-6

A_HEADS = 8
A_HEAD_DIM = 64
A_WIDTH = A_HEADS * A_HEAD_DIM
N_PREV_CHUNKS = 8
BAND = (N_PREV_CHUNKS + 1) * CHUNK
REL_CLIP = 128
N_REL = 2 * REL_CLIP + 1

B_HEADS = 4
B_KEY_DIM = 64
B_VAL_DIM = 128
B_KEY_WIDTH = B_HEADS * B_KEY_DIM
B_WIDTH = B_HEADS * B_VAL_DIM
GATE_RANK = 16
GATE_TAU = 16.0

N_BRANCH = 2
BRANCH_WIDTH = 512

D_FF = 4 * D_MODEL

IN_SPLITS = (A_WIDTH, A_WIDTH, A_WIDTH, B_KEY_WIDTH, B_KEY_WIDTH, B_WIDTH, B_WIDTH, GATE_RANK, D_MODEL, D_MODEL)
IN_COLS = 5136

kernel_name = "hybrid_bandattn_gla_gated_merge"


def rmsnorm(x, g):
    xf = x.astype(jnp.float32)
    y = xf * lax.rsqrt(jnp.mean(xf * xf, axis=-1, keepdims=True) + EPS)
    return (y * g.astype(jnp.float32)).astype(x.dtype)


def _rel_index():
    i = np.arange(CHUNK)[:, None]
    kk = np.arange(BAND)[None, :]
    dist = N_PREV_CHUNKS * CHUNK + i - kk
    return (np.clip(dist, -REL_CLIP, REL_CLIP) + REL_CLIP).astype(np.int32)


def band_attention(q, k, v, rel_bias):
    Bsz, S, H, Dh = q.shape
    n_c = S // CHUNK
    qc = q.reshape(Bsz, n_c, CHUNK, H, Dh)
    pad = ((0, 0), (N_PREV_CHUNKS, 0), (0, 0), (0, 0), (0, 0))
    kp = jnp.pad(k.reshape(Bsz, n_c, CHUNK, H, Dh), pad)
    vp = jnp.pad(v.reshape(Bsz, n_c, CHUNK, H, Dh), pad)
    band_idx = jnp.arange(n_c)[:, None] + jnp.arange(N_PREV_CHUNKS + 1)[None, :]
    kb = kp[:, band_idx].reshape(Bsz, n_c, BAND, H, Dh)
    vb = vp[:, band_idx].reshape(Bsz, n_c, BAND, H, Dh)
    s = jnp.einsum('bcqhd,bckhd->bhcqk', qc, kb).astype(jnp.float32) * (Dh ** -0.5)
    bias = rel_bias.astype(jnp.float32)[:, _rel_index()]
    s = s + bias[:, None, :, :]
    key_chunk = jnp.arange(n_c)[:, None] - N_PREV_CHUNKS + (jnp.arange(BAND) // CHUNK)[None, :]
    valid = key_chunk >= 0
    s = jnp.where(valid[None, None, :, None, :], s, -1e30)
    p = jax.nn.softmax(s, axis=-1).astype(v.dtype)
    o = jnp.einsum('bhcqk,bckhd->bcqhd', p, vb)
    return o.reshape(Bsz, S, H * Dh)


def gla(q, k, v, log_a):
    f32 = jnp.float32
    Bsz, S, H, Dk = q.shape
    Dv = v.shape[-1]
    n_c = S // CHUNK
    qc = q.reshape(Bsz, n_c, CHUNK, H, Dk).astype(f32) * (Dk ** -0.5)
    kc = k.reshape(Bsz, n_c, CHUNK, H, Dk).astype(f32)
    vc = v.reshape(Bsz, n_c, CHUNK, H, Dv).astype(f32)
    b = jnp.cumsum(log_a.reshape(Bsz, n_c, CHUNK, H, Dk).astype(f32), axis=2)
    b_last = b[:, :, -1]
    q_t = qc * jnp.exp(b)
    k_t = kc * jnp.exp(-b)
    k_end = kc * jnp.exp(b_last[:, :, None] - b)
    att = jnp.einsum('bcihd,bcjhd->bchij', q_t, k_t)
    causal = jnp.tril(jnp.ones((CHUNK, CHUNK), dtype=bool))
    att = jnp.where(causal, att, 0.0)
    o_intra = jnp.einsum('bchij,bcjhv->bcihv', att, vc)
    d_state = jnp.einsum('bcjhd,bcjhv->bchdv', k_end, vc)

    def step(state, inp):
        ds_c, decay_c = inp
        return decay_c[..., None] * state + ds_c, state

    s0 = jnp.zeros((Bsz, H, Dk, Dv), f32)
    _, s_prev = lax.scan(step, s0, (jnp.swapaxes(d_state, 0, 1), jnp.swapaxes(jnp.exp(b_last), 0, 1)))
    s_prev = jnp.swapaxes(s_prev, 0, 1)
    o_inter = jnp.einsum('bcihd,bchdv->bcihv', q_t, s_prev)
    return (o_intra + o_inter).reshape(Bsz, S, H, Dv)


def hybrid_mixer(h, w_in, rel_bias, w_gate_lr, b_gate, gla_norm_g, w_branch, w_out):
    Bsz, S, _ = h.shape
    proj = h @ w_in
    split_at = np.cumsum(np.array(IN_SPLITS))[:-1].tolist()
    qa, ka, va, qb, kb, vb, rb, lrb, ga, gb = jnp.split(proj, split_at, axis=-1)
    y_a = band_attention(qa.reshape(Bsz, S, A_HEADS, A_HEAD_DIM),
                         ka.reshape(Bsz, S, A_HEADS, A_HEAD_DIM),
                         va.reshape(Bsz, S, A_HEADS, A_HEAD_DIM), rel_bias)
    log_a = jax.nn.log_sigmoid((lrb @ w_gate_lr + b_gate).astype(jnp.float32)) / GATE_TAU
    o_b = gla(qb.reshape(Bsz, S, B_HEADS, B_KEY_DIM),
              kb.reshape(Bsz, S, B_HEADS, B_KEY_DIM),
              vb.reshape(Bsz, S, B_HEADS, B_VAL_DIM),
              log_a.reshape(Bsz, S, B_HEADS, B_KEY_DIM))
    o_b = o_b * lax.rsqrt(jnp.mean(o_b * o_b, axis=-1, keepdims=True) + EPS)
    o_b = o_b * gla_norm_g.astype(jnp.float32).reshape(B_HEADS, B_VAL_DIM)
    y_b = (o_b.reshape(Bsz, S, B_WIDTH) * jax.nn.silu(rb.astype(jnp.float32))).astype(h.dtype)
    u_a = y_a @ w_branch[0]
    u_b = y_b @ w_branch[1]
    merged = jax.nn.sigmoid(ga) * u_a + jax.nn.sigmoid(gb) * u_b
    return merged @ w_out


def sqrelu_mlp(h, w_up, w_down):
    return jnp.square(jax.nn.relu(h @ w_up)) @ w_down


def setup_inputs(seed: int = 0) -> dict:
    key = jax.random.key(seed)
    ks = jax.random.split(key, 13)

    def nrm(k, shape, scale):
        return scale * jax.random.normal(k, shape, jnp.float32)

    return {
        "x": nrm(ks[0], (BATCH, SEQ, D_MODEL), 1.0),
        "mix_norm_g": 1.0 + nrm(ks[1], (DEPTH, D_MODEL), 0.02),
        "w_in": nrm(ks[2], (DEPTH, D_MODEL, IN_COLS), D_MODEL ** -0.5),
        "rel_bias": nrm(ks[3], (DEPTH, A_HEADS, N_REL), 0.1),
        "w_gate_lr": nrm(ks[4], (DEPTH, GATE_RANK, B_KEY_WIDTH), GATE_RANK ** -0.5),
        "b_gate": nrm(ks[5], (DEPTH, B_KEY_WIDTH), 0.5),
        "gla_norm_g": 1.0 + nrm(ks[6], (DEPTH, B_WIDTH), 0.02),
        "w_branch": nrm(ks[7], (DEPTH, N_BRANCH, BRANCH_WIDTH, D_MODEL), BRANCH_WIDTH ** -0.5),
        "w_out": nrm(ks[8], (DEPTH, D_MODEL, D_MODEL), D_MODEL ** -0.5),
        "mlp_norm_g": 1.0 + nrm(ks[9], (DEPTH, D_MODEL), 0.02),
        "w_up": nrm(ks[10], (DEPTH, D_MODEL, D_FF), D_MODEL ** -0.5),
        "w_down": nrm(ks[11], (DEPTH, D_FF, D_MODEL), D_FF ** -0.5),
        "final_norm_g": 1.0 + nrm(ks[12], (D_MODEL,), 0.02),
    }


def reference(x, mix_norm_g, w_in, rel_bias, w_gate_lr, b_gate, gla_norm_g, w_branch, w_out,
              mlp_norm_g, w_up, w_down, final_norm_g):
    res = x
    for l in range(DEPTH):
        h = rmsnorm(res, mix_norm_g[l])
        res = res + hybrid_mixer(h, w_in[l], rel_bias[l], w_gate_lr[l], b_gate[l], gla_norm_g[l],
                                 w_branch[l], w_out[l])
        h = rmsnorm(res, mlp_norm_g[l])
        res = res + sqrelu_mlp(h, w_up[l], w_down[l])
    return rmsnorm(res, final_norm_g)
```

```python
import contextlib
import numpy as np
import concourse.bass as bass
import concourse.mybir as mybir
from concourse.bass_utils import run_bass_kernel_spmd

F32 = mybir.dt.float32
BF16 = mybir.dt.bfloat16
AF = mybir.ActivationFunctionType
ALU = mybir.AluOpType

D = 1024
SEQ = 2048
DEPTH = 4
NCORES = 8
IN_COLS = 5136
EPS = 1e-6
C_QA, C_KA, C_VA, C_QB, C_KB, C_VB, C_RB, C_LR, C_GA, C_GB = 0, 512, 1024, 1536, 1792, 2048, 2560, 3072, 3088, 4112


class Op:
    __slots__ = ("eng", "fn", "deps", "is_dma", "signals", "sem", "val", "epoch", "prev_dma")

    def __init__(self, eng, fn, is_dma, epoch):
        self.eng = eng
        self.fn = fn
        self.is_dma = is_dma
        self.deps = []
        self.signals = False
        self.sem = None
        self.val = 0
        self.epoch = epoch
        self.prev_dma = None


class Sched:
    N_DMA_SEMS = 8

    def __init__(self, nc):
        self.nc = nc
        self.engs = {"pe": nc.tensor, "act": nc.scalar, "dve": nc.vector, "pool": nc.gpsimd, "sp": nc.sync}
        self.ops = {e: [] for e in self.engs}
        self.last_writer = {}
        self.readers = {}
        self.epoch = 0
        self.pending_barrier = {}

    def barrier(self, include_pool=False):
        last = [lst[-1] for e, lst in self.ops.items() if lst and e in ("pe", "act", "dve")]
        for e in self.engs:
            if e == "pool" and not include_pool:
                continue
            if e == "pe":
                continue
            self.pending_barrier[e] = last

    def add(self, eng, fn, reads=(), writes=(), dma=False):
        op = Op(eng, fn, dma, self.epoch)
        deps = {}

        def add_dep(d, raw):
            if d is op:
                return
            if (not d.is_dma) and d.eng == eng and not dma:
                if eng == "pe":
                    return
            deps[id(d)] = d

        for k in reads:
            w = self.last_writer.get(k)
            if w is not None:
                add_dep(w, True)
        for k in writes:
            w = self.last_writer.get(k)
            if w is not None:
                add_dep(w, False)
            rd = self.readers.get(k)
            if rd:
                for r in rd.values():
                    add_dep(r, False)
        pb = self.pending_barrier.pop(eng, None)
        if pb:
            for d in pb:
                if d.eng != eng or dma:
                    deps[id(d)] = d
        op.deps = list(deps.values())
        for k in reads:
            rd = self.readers.setdefault(k, {})
            if dma:
                rd[("dma", id(op))] = op
            else:
                rd[eng] = op
        for k in writes:
            self.last_writer[k] = op
            self.readers[k] = {}
        self.ops[eng].append(op)
        return op

    def emit(self, final_wait_ops=()):
        nc = self.nc
        for e, lst in self.ops.items():
            for op in lst:
                for d in op.deps:
                    d.signals = True
        n_epochs = self.epoch + 1
        with contextlib.ExitStack() as st:
            csem = {}
            for e in self.engs:
                for ep in range(n_epochs):
                    csem[(e, ep)] = st.enter_context(nc.semaphore(f"c_{e}_{ep}"))
            dsem = {}
            for e in self.engs:
                if any(op.is_dma for op in self.ops[e]):
                    dsem[e] = [st.enter_context(nc.semaphore(f"d_{e}_{i}")) for i in range(self.N_DMA_SEMS)]
            for e, lst in self.ops.items():
                cnt = {}
                dcnt = [0] * self.N_DMA_SEMS
                dlast = [None] * self.N_DMA_SEMS
                nd = 0
                for op in lst:
                    if op.is_dma:
                        k = nd % self.N_DMA_SEMS
                        nd += 1
                        dcnt[k] += 16
                        op.sem = dsem[e][k]
                        op.val = dcnt[k]
                        op.prev_dma = dlast[k]
                        dlast[k] = op
                    elif op.signals:
                        cnt[op.epoch] = cnt.get(op.epoch, 0) + 1
                        op.sem = csem[(e, op.epoch)]
                        op.val = cnt[op.epoch]
            block = st.enter_context(nc.Block())
            for e in self.engs:
                self._emit_engine(block, e, final_wait_ops if e == "sp" else ())

    def _emit_engine(self, block, e, final_wait_ops):
        lst = self.ops[e]
        if not lst and not final_wait_ops:
            return
        reg = {"pe": block.tensor, "act": block.scalar, "dve": block.vector, "pool": block.gpsimd, "sp": block.sync}[e]

        @reg
        def _(eng):
            waited = {}

            def wait(sem, val):
                key = id(sem)
                if waited.get(key, 0) < val:
                    eng.wait_ge(sem, val)
                    waited[key] = val

            for op in lst:
                for d in op.deps:
                    wait(d.sem, d.val)
                if op.is_dma and op.prev_dma is not None:
                    wait(op.prev_dma.sem, op.prev_dma.val)
                ins = op.fn(eng)
                if op.is_dma:
                    ins.then_inc(op.sem, 16)
                elif op.signals:
                    ins.then_inc(op.sem, 1)
            for d in final_wait_ops:
                wait(d.sem, d.val)


def build(n_layers=DEPTH, final_norm=True, tap=None):
    nc = bass.Bass("TRN2", target_bir_lowering=False)
    L = n_layers

    def din(name, shape):
        return nc.dram_tensor(name, shape, F32, kind="ExternalInput").ap()

    xT = din("xT", [D, SEQ])
    w_in = din("w_in", [L, D, IN_COLS])
    w_branch = din("w_branch", [L, 2, 512, D])
    w_out = din("w_out", [L, D, D])
    w_up = din("w_up", [L, D, 4 * D])
    w_down = din("w_down", [L, 4 * D, D])
    wg_d = din("wg", [L, 16, 256])
    bg_d = din("bg", [L, 128, 256])
    mh_d = din("mh", [L, 128, 8 * 256])
    vec_d = din("vecs", [128, 128])
    cst_d = din("csts", [128, 5 * 128])
    if tap is None:
        outT = nc.dram_tensor("outT", [D, SEQ], F32, kind="ExternalOutput").ap()
    else:
        outT = nc.dram_tensor("outT", [D, SEQ], F32, kind="ExternalOutput").ap()

    V_MIX, V_MLP, V_FIN, V_GLA, V_CB = 0, 32, 64, 72, 88

    base = (nc.sbuf_base + 63) // 64 * 64
    top = nc.sbuf_top
    cur = [base]

    def alloc(name, shape, dt, at=None):
        sz = int(np.prod(shape[1:])) * (4 if dt == F32 else 2)
        sz = (sz + 63) // 64 * 64
        if at is None:
            o = cur[0]
            cur[0] += sz
        else:
            o = at
        assert o + sz <= top, (name, o, sz, top)
        return nc.alloc_sbuf_tensor_at(name, shape, dt, offset=o), o + sz

    resT, _ = alloc("resT", [128, 8, SEQ], F32)
    hT, _ = alloc("hT", [128, 8, SEQ], BF16)
    NB = 4
    wbf, _ = alloc("wbf", [128, NB, 2048], BF16)
    vecs, _ = alloc("vecs", [128, 128], F32)
    csts, _ = alloc("csts", [128, 5 * 128], F32)
    csts_b, _ = alloc("csts_b", [128, 5 * 128], BF16)
    lrbW, _ = alloc("lrbW", [128, 8, 16], BF16)
    scratch = cur[0]
    SCR = top - scratch
    assert SCR >= 86000, SCR

    class Region:
        def __init__(self, start):
            self.o = start

        def a(self, name, shape, dt):
            t, e = alloc(name, shape, dt, at=self.o)
            self.o = e
            return t

    y_bT, _ = alloc("y_bT", [128, 4, SEQ], BF16, at=scratch)
    y_aT, _ = alloc("y_aT", [128, 4, SEQ], BF16, at=scratch + 16384)
    R2 = scratch + 32768
    rn = Region(top - 20480 - 64)
    n_sq = [rn.a(f"n_sq{i}", [128, 8, 512], BF16) for i in range(2)]
    n_ln = [rn.a(f"n_ln{i}", [128, 512], F32) for i in range(2)]
    rn2 = Region(scratch + 12288)
    n2_sq = [rn2.a(f"n2_sq{i}", [128, 8, 512], BF16) for i in range(2)]
    n2_ln = [rn2.a(f"n2_ln{i}", [128, 512], F32) for i in range(2)]
    assert rn2.o <= scratch + 32768
    rg = Region(scratch + 16384)
    g_lrbT = [rg.a(f"g_lrbT{i}", [128, 512], F32) for i in range(2)]
    g_la = rg.a("g_la", [128, 4, 256], F32)
    g_z = [rg.a(f"g_z{i}", [128, 256], F32) for i in range(4)]
    g_eb = rg.a("g_eb", [128, 2, 512], F32)
    g_enb = rg.a("g_enb", [128, 2, 512], F32)
    g_eend = rg.a("g_eend", [128, 4, 256], F32)
    g_dec = rg.a("g_dec", [128, 2, 32], F32)
    g_qt = rg.a("g_qt", [128, 4, 512], BF16)
    g_kt = rg.a("g_kt", [128, 2, 512], BF16)
    g_kend = rg.a("g_kend", [128, 4, 2, 256], BF16)
    g_vb = rg.a("g_vb", [128, 4, 512], BF16)
    g_th = [rg.a(f"g_th{i}", [128, 512], F32) for i in range(2)]
    g_srb = rg.a("g_srb", [128, 4, 512], BF16)
    g_S2 = rg.a("g_S2", [128, 2, 2, 256], F32)
    g_S2b = rg.a("g_S2b", [128, 2, 8, 256], BF16)
    g_att = [rg.a(f"g_att{i}", [128, 128], BF16) for i in range(16)]
    g_sq = [rg.a(f"g_sq{i}", [128, 512], BF16) for i in range(2)]
    g_rs = [rg.a(f"g_rs{i}", [128, 512], F32) for i in range(2)]
    g_wgt = [rg.a(f"g_wgt{i}", [128, 512], F32) for i in range(2)]
    g_bg = rg.a("g_bg", [128, 256], F32)
    g_wg = rg.a("g_wg", [128, 256], F32)
    assert rg.o <= top
    ra = Region(R2)
    a_q = [ra.a(f"a_q{i}", [128, SEQ], BF16) for i in range(2)]
    a_k = [ra.a(f"a_k{i}", [128, 2, SEQ], BF16) for i in range(2)]
    a_v = [ra.a(f"a_v{i}", [128, 16, 2, 128], BF16) for i in range(2)]
    a_mh = ra.a("a_mh", [128, 8, 256], F32)
    a_pT = [ra.a(f"a_pT{i}", [128, 512], BF16) for i in range(4)]
    a_t = [ra.a(f"a_t{i}", [128, 256], F32) for i in range(3)]
    a_rc = [ra.a(f"a_rc{i}", [128, 512], F32) for i in range(2)]
    assert ra.o <= top
    rm = Region(R2)
    m_mg = rm.a("m_mg", [128, 8, SEQ], BF16)
    m_th = [rm.a(f"m_th{i}", [128, 512], F32) for i in range(4)]
    m_t = [rm.a(f"m_t{i}", [128, 512], F32) for i in range(4)]
    NBX = 2
    m_wx = rm.a("m_wx", [128, NBX, 2048], BF16)
    assert rm.o <= top
    rl = Region(scratch)
    l_a = rl.a("l_a", [128, 16, SEQ], BF16)
    l_r = [rl.a(f"l_r{i}", [128, 512], F32) for i in range(3)]
    l_sq = [rl.a(f"l_sq{i}", [128, 512], BF16) for i in range(6)]
    assert rl.o <= top
    rf = Region(R2)
    f_o = [rf.a(f"f_o{i}", [128, 512], F32) for i in range(4)]

    ps = [nc.alloc_psum_tensor(f"ps{i}", [128, 512], F32) for i in range(8)]

    S = Sched(nc)

    rot = {}

    def nxt(name, n):
        i = rot.get(name, 0)
        rot[name] = i + 1
        return i % n

    def psum(pool):
        key = tuple(pool)
        return pool[nxt(("ps", key), len(pool))]

    ALLB = list(range(8))

    wx_mode = [False]

    def wslot_ap(slot):
        if slot >= NB:
            return m_wx[:, slot - NB, :]
        return wbf[:, slot, :]

    def wload(parts):
        if wx_mode[0]:
            slot = [0, 1, 2, 3, NB, NB + 1][nxt("wslotx", NB + NBX)] if NB == 4 else nxt("wslot", NB)
        else:
            slot = nxt("wslot", NB)
        key = ("w", slot)
        rd = ["mrg_gate"] if slot >= NB else []
        for (o, nk, ncol, ap) in parts:
            dst = wslot_ap(slot)[:, o:o + nk * ncol].rearrange("p (k c) -> p k c", c=ncol)
            S.add("pool", lambda e, dst=dst, ap=ap: e.dma_start(out=dst, in_=ap), reads=rd, writes=[key], dma=True)
        return slot, key

    def wview(slot, nk, ncol, o=0):
        return wslot_ap(slot)[:, o:o + nk * ncol].rearrange("p (k c) -> p k c", c=ncol)

    def win_ap(l, c0, ncol):
        return w_in[l, :, c0:c0 + ncol].rearrange("(k p) c -> p k c", p=128)

    def mm_group(out_ap, pairs, reads, writes):
        n = len(pairs)

        def fn(e):
            ins = None
            for i, (lt, rh) in enumerate(pairs):
                ins = e.matmul(out_ap, lhsT=lt, rhs=rh, start=(i == 0), stop=(i == n - 1))
            return ins

        return S.add("pe", fn, reads=reads, writes=writes)

    def tb_sl(tb):
        return slice(tb * 512, (tb + 1) * 512)

    S.add("sp", lambda e: e.dma_start(out=vecs[:], in_=vec_d[:, :]), writes=["vecs"], dma=True)
    S.add("sp", lambda e: e.dma_start(out=csts[:], in_=cst_d[:, :]), writes=["csts"], dma=True)
    S.add("dve", lambda e: e.tensor_copy(out=csts_b[:], in_=csts[:]), reads=["csts"], writes=["csts_b"])
    ones_mean = csts_b[:, 0:128]
    ones_hd = csts_b[:, 128:256]
    U_f = csts[:, 256:384]
    R_f = csts[:, 384:512]
    maskT = csts[:, 512:640]
    xv = xT.rearrange("(c p) t -> p c t", p=128)
    for tb in range(4):
        S.add("sp", lambda e, tb=tb: e.dma_start(out=resT[:, :, tb_sl(tb)], in_=xv[:, :, tb_sl(tb)]),
              writes=[("res", c, tb) for c in range(8)], dma=True)

    def rmsnorm(gcol, dst_fn, dst_keys_fn, tbs=(0, 1, 2, 3), temps=None):
        t_sq, t_ln, t_name = temps if temps is not None else (n_sq, n_ln, "n")
        for tb in tbs:
            sq = t_sq[tb % 2]
            sqk = (t_name + "_sq", tb % 2)
            for c in range(8):
                S.add("act", lambda e, c=c, tb=tb, sq=sq: e.activation(out=sq[:, c, :], in_=resT[:, c, tb_sl(tb)], func=AF.Square),
                      reads=[("res", c, tb)], writes=[(sqk, c)])
            b = psum(ALLB)
            mm_group(ps[b][:], [(ones_mean, sq[:, c, :]) for c in range(8)],
                     reads=[(sqk, c) for c in range(8)] + ["csts_b"], writes=[("ps", b)])
            ln = t_ln[tb % 2]
            lnk = (t_name + "_ln", tb % 2)
            S.add("act", lambda e, b=b, ln=ln: e.activation(out=ln[:], in_=ps[b][:], func=AF.Ln, bias=EPS),
                  reads=[("ps", b)], writes=[lnk])
            S.add("act", lambda e, ln=ln: e.activation(out=ln[:], in_=ln[:], func=AF.Exp, scale=-0.5),
                  reads=[lnk], writes=[lnk])
            for c in range(8):
                S.add("dve", lambda e, c=c, tb=tb, ln=ln: e.scalar_tensor_tensor(
                    out=dst_fn(c, tb), in0=resT[:, c, tb_sl(tb)], scalar=vecs[:, gcol + c:gcol + c + 1], in1=ln[:],
                    op0=ALU.mult, op1=ALU.mult),
                    reads=[("res", c, tb), lnk, "vecs"], writes=dst_keys_fn(c, tb))

    def norm_tail(gcol, emit_out):
        for tb in range(4):
            b = 4 + tb
            ln = n_ln[tb % 2]
            lnk = ("n_ln", tb % 2)
            S.add("act", lambda e, b=b, ln=ln: e.activation(out=ln[:], in_=ps[b][:], func=AF.Ln, bias=EPS),
                  reads=[("ps", b)], writes=[lnk])
            S.add("act", lambda e, ln=ln: e.activation(out=ln[:], in_=ln[:], func=AF.Exp, scale=-0.5),
                  reads=[lnk], writes=[lnk])
            for c in range(8):
                emit_out(c, tb, ln, lnk, gcol)

    def h_out(c, tb, ln, lnk, gcol):
        S.add("dve", lambda e: e.scalar_tensor_tensor(
            out=hT[:, c, tb_sl(tb)], in0=resT[:, c, tb_sl(tb)], scalar=vecs[:, gcol + c:gcol + c + 1], in1=ln[:],
            op0=ALU.mult, op1=ALU.mult),
            reads=[("res", c, tb), lnk, "vecs"], writes=[("h", c, tb)])

    def h_dst(c, tb):
        return hT[:, c, tb_sl(tb)]

    def h_keys(c, tb):
        return [("h", c, tb)]

    def h_reads(tb):
        return [("h", c, tb) for c in range(8)]

    def gla_phase(l):
        S.add("sp", lambda e: e.dma_start(out=g_bg[:], in_=bg_d[l, :, :]), writes=["g_bg"], dma=True)
        S.add("sp", lambda e: e.dma_start(out=g_wg[0:16, :], in_=wg_d[l, :, :]), writes=["g_wg"], dma=True)
        S.add("pool", lambda e: e.dma_start(out=lrbW[:], in_=win_ap(l, C_LR, 16)), writes=["lrbW"], dma=True)
        S.add("dve", lambda e: e.memset(g_S2[:], 0.0), writes=[("S2", 0, 0), ("S2", 0, 1), ("S2", 1, 0), ("S2", 1, 1)])
        for h in range(4):
            zr0 = 64 * (1 - h % 2)
            S.add("dve", lambda e, h=h, zr0=zr0: e.memset(g_qt[zr0:zr0 + 64, h, :], 0.0), writes=[("g_qt", h)])
        for p in range(2):
            zr0 = 64 * (1 - p)
            S.add("dve", lambda e, p=p, zr0=zr0: e.memset(g_kend[zr0:zr0 + 64, :, p, :], 0.0), writes=[("g_kend", tt, p) for tt in range(4)])
        MMB = [0, 1, 2, 3]
        for tb in range(4):
            hr = h_reads(tb)
            b = psum(MMB)
            mm_group(ps[b][0:16, :], [(lrbW[:, kc, :], hT[:, kc, tb_sl(tb)]) for kc in range(8)],
                     reads=hr + ["lrbW"], writes=[("ps", b)])
            lrbT = g_lrbT[tb % 2]
            lk = ("g_lrbT", tb % 2)
            S.add("act", lambda e, b=b, lrbT=lrbT: e.activation(out=lrbT[0:16, :], in_=ps[b][0:16, :], func=AF.Copy),
                  reads=[("ps", b)], writes=[lk])
            sl, wk = wload([(0, 8, 256, win_ap(l, C_VB + 0 * 256, 256))])
            W = wview(sl, 8, 256)
            for tt in range(4):
                b = psum(MMB)
                t0 = tb * 512 + tt * 128
                mm_group(ps[b][:, 0:256], [(hT[:, kc, t0:t0 + 128], W[:, kc, :]) for kc in range(8)],
                         reads=hr + [wk], writes=[("ps", b)])
                S.add("act", lambda e, b=b, tt=tt, u=0: e.activation(out=g_vb[:, tt, u * 256:(u + 1) * 256], in_=ps[b][:, 0:256], func=AF.Copy),
                      reads=[("ps", b)], writes=[("g_vb", tt, 0)])
            bB = [4, 5]
            zb = []
            for tt in range(4):
                b = 6 + tt // 2
                zb.append(b)
                S.add("pe", lambda e, b=b, tt=tt, lrbT=lrbT: e.matmul(ps[b][:, (tt % 2) * 256:(tt % 2) * 256 + 256], lhsT=lrbT[0:16, tt * 128:(tt + 1) * 128],
                                                                      rhs=g_wg[0:16, :], start=True, stop=True),
                      reads=[lk, "g_wg"], writes=[("ps", b)])
            for tt in range(4):
                S.add("dve", lambda e, b=zb[tt], tt=tt: e.tensor_tensor(out=g_z[tt][:], in0=ps[b][:, (tt % 2) * 256:(tt % 2) * 256 + 256], in1=g_bg[:], op=ALU.add),
                      reads=[("ps", zb[tt]), "g_bg"], writes=[("g_z", tt)])
            for tt in range(4):
                S.add("act", lambda e, tt=tt: e.activation(out=g_z[tt][:], in_=g_z[tt][:], func=AF.Exp, scale=-1.0),
                      reads=[("g_z", tt)], writes=[("g_z", tt)])
            for tt in range(4):
                S.add("act", lambda e, tt=tt: e.activation(out=g_la[:, tt, :], in_=g_z[tt][:], func=AF.Ln, bias=1.0),
                      reads=[("g_z", tt)], writes=[("g_la", tt)])
            sl, wk = wload([(0, 8, 256, win_ap(l, C_VB + 1 * 256, 256))])
            W = wview(sl, 8, 256)
            for tt in range(4):
                b = psum(MMB)
                t0 = tb * 512 + tt * 128
                mm_group(ps[b][:, 0:256], [(hT[:, kc, t0:t0 + 128], W[:, kc, :]) for kc in range(8)],
                         reads=hr + [wk], writes=[("ps", b)])
                S.add("act", lambda e, b=b, tt=tt, u=1: e.activation(out=g_vb[:, tt, u * 256:(u + 1) * 256], in_=ps[b][:, 0:256], func=AF.Copy),
                      reads=[("ps", b)], writes=[("g_vb", tt, 1)])
            rb_ = []
            for tt in range(4):
                lak = ("g_la", tt)
                for dc in range(2):
                    S.add("pe", lambda e, dc=dc, tt=tt: e.matmul(ps[bB[dc]][:, tt * 128:(tt + 1) * 128],
                                                                 lhsT=g_la[:, tt, dc * 128:(dc + 1) * 128], rhs=U_f,
                                                                 start=True, stop=True),
                          reads=[lak, "csts"], writes=[("ps", bB[dc])])
                b2 = 6 + tt // 2
                rb_.append(b2)
                S.add("pe", lambda e, b2=b2, tt=tt: e.matmul(ps[b2][:, (tt % 2) * 256:(tt % 2) * 256 + 256], lhsT=R_f, rhs=g_la[:, tt, :], start=True, stop=True),
                      reads=[lak, "csts"], writes=[("ps", b2)])
            for u in range(2):
                sl, wk = wload([(0, 8, 256, win_ap(l, C_RB + u * 256, 256))])
                W = wview(sl, 8, 256)
                for hh in range(2):
                    h = 2 * u + hh
                    b = psum(MMB)
                    mm_group(ps[b][:], [(W[:, kc, hh * 128:(hh + 1) * 128], hT[:, kc, tb_sl(tb)]) for kc in range(8)],
                             reads=hr + [wk], writes=[("ps", b)])
                    th = g_th[h % 2]
                    thk = ("g_th", h % 2)
                    S.add("act", lambda e, b=b, th=th: e.activation(out=th[:], in_=ps[b][:], func=AF.Tanh, scale=0.5),
                          reads=[("ps", b)], writes=[thk])
                    S.add("dve", lambda e, b=b, th=th, h=h: e.scalar_tensor_tensor(out=g_srb[:, h, :], in0=th[:], scalar=1.0, in1=ps[b][:],
                                                                                  op0=ALU.add, op1=ALU.mult),
                          reads=[("ps", b), thk], writes=[("g_srb", h)])
            for tt in range(4):
                S.add("act", lambda e, b2=rb_[tt], tt=tt: e.activation(out=g_eend[:, tt, :], in_=ps[b2][:, (tt % 2) * 256:(tt % 2) * 256 + 256], func=AF.Exp, scale=-1.0 / 16),
                      reads=[("ps", rb_[tt])], writes=[("g_eend", tt)])
            for dc in range(2):
                S.add("act", lambda e, dc=dc: e.activation(out=g_eb[:, dc, :], in_=ps[bB[dc]][:], func=AF.Exp, scale=-1.0 / 16),
                      reads=[("ps", bB[dc])], writes=[("g_eb", dc)])
                S.add("act", lambda e, dc=dc: e.activation(out=g_enb[:, dc, :], in_=ps[bB[dc]][:], func=AF.Exp, scale=1.0 / 16),
                      reads=[("ps", bB[dc])], writes=[("g_enb", dc)])
                S.add("act", lambda e, dc=dc, tb=tb: e.activation(out=g_dec[:, dc, tb * 8:(tb + 1) * 8], in_=ps[bB[dc]][:, 63::64],
                                                                  func=AF.Exp, scale=-1.0 / 16),
                      reads=[("ps", bB[dc])], writes=[("g_dec", dc)])
            sl, wk = wload([(0, 8, 256, win_ap(l, C_QB, 256))])
            W = wview(sl, 8, 256)
            for dc in range(2):
                b = psum(MMB)
                mm_group(ps[b][:], [(W[:, kc, dc * 128:(dc + 1) * 128], hT[:, kc, tb_sl(tb)]) for kc in range(8)],
                         reads=hr + [wk], writes=[("ps", b)])
                for hl in range(2):
                    S.add("dve", lambda e, b=b, dc=dc, hl=hl: e.scalar_tensor_tensor(
                        out=g_qt[64 * hl:64 * hl + 64, 2 * dc + hl, :], in0=ps[b][64 * hl:64 * hl + 64, :], scalar=0.125,
                        in1=g_eb[64 * hl:64 * hl + 64, dc, :], op0=ALU.mult, op1=ALU.mult),
                        reads=[("ps", b), ("g_eb", dc)], writes=[("g_qt", 2 * dc + hl)])
            sl, wk = wload([(0, 8, 256, win_ap(l, C_KB, 256))])
            W = wview(sl, 8, 256)
            for dc in range(2):
                b = psum(MMB)
                mm_group(ps[b][:], [(W[:, kc, dc * 128:(dc + 1) * 128], hT[:, kc, tb_sl(tb)]) for kc in range(8)],
                         reads=hr + [wk], writes=[("ps", b)])
                S.add("dve", lambda e, b=b, dc=dc: e.tensor_tensor(out=g_kt[:, dc, :], in0=ps[b][:], in1=g_enb[:, dc, :], op=ALU.mult),
                      reads=[("ps", b), ("g_enb", dc)], writes=[("g_kt", dc)])
            for tt in range(4):
                b = psum(MMB)
                t0 = tb * 512 + tt * 128
                mm_group(ps[b][:, 0:256], [(hT[:, kc, t0:t0 + 128], W[:, kc, :]) for kc in range(8)],
                         reads=hr + [wk], writes=[("ps", b)])
                for p in range(2):
                    S.add("dve", lambda e, b=b, tt=tt, p=p: e.tensor_tensor(
                        out=g_kend[64 * p:64 * p + 64, tt, p, :], in0=ps[b][64 * p:64 * p + 64, 0:256], in1=g_eend[64 * p:64 * p + 64, tt, :], op=ALU.mult),
                        reads=[("ps", b), ("g_eend", tt)], writes=[("g_kend", tt, p)])
            OB = [4, 5, 6, 7]
            for tt in range(4):
                b = psum(MMB)
                for h in range(4):
                    dc, hl = h // 2, h % 2
                    pb = 64 * hl
                    S.add("pe", lambda e, b=b, dc=dc, tt=tt, h=h: e.matmul(
                        ps[b][:, h * 128:(h + 1) * 128], lhsT=g_kt[:, dc, tt * 128:(tt + 1) * 128],
                        rhs=g_qt[:, h, tt * 128:(tt + 1) * 128], start=True, stop=True),
                        reads=[("g_kt", dc), ("g_qt", h)], writes=[("ps", b)])
                for h in range(4):
                    att = g_att[tt * 4 + h]
                    S.add("dve", lambda e, b=b, att=att, h=h: e.tensor_tensor(out=att[:], in0=ps[b][:, h * 128:(h + 1) * 128], in1=maskT, op=ALU.mult),
                          reads=[("ps", b), "csts"], writes=[("g_att", tt * 4 + h)])
            for cc in range(8):
                c = tb * 8 + cc
                tt, p = cc // 2, cc % 2
                for dc in range(2):
                    src, dst = c % 2, (c + 1) % 2
                    S.add("act", lambda e, dc=dc, cc=cc, src=src: e.activation(out=g_S2b[:, dc, cc, :], in_=g_S2[:, dc, src, :], func=AF.Copy),
                          reads=[("S2", dc, src)], writes=[("S2b", dc, cc)])
                    b = psum(MMB)
                    S.add("pe", lambda e, b=b, dc=dc, p=p, tt=tt: e.matmul(
                        ps[b][:, 0:256], lhsT=g_kend[:, tt, p, dc * 128:(dc + 1) * 128],
                        rhs=g_vb[:, tt, dc * 256:(dc + 1) * 256], start=True, stop=True),
                        reads=[("g_kend", tt, p), ("g_vb", tt, dc)], writes=[("ps", b)])
                    S.add("dve", lambda e, b=b, dc=dc, c=c, src=src, dst=dst: e.scalar_tensor_tensor(
                        out=g_S2[:, dc, dst, :], in0=g_S2[:, dc, src, :], scalar=g_dec[:, dc, c:c + 1], in1=ps[b][:, 0:256],
                        op0=ALU.mult, op1=ALU.add),
                        reads=[("ps", b), ("S2", dc, src), ("g_dec", dc)], writes=[("S2", dc, dst)])
                o_tts = []
                if cc % 2 == 1 and cc >= 3:
                    o_tts.append((cc - 3) // 2)
                if cc == 7:
                    o_tts.append(3)
                for tt in o_tts:
                    for h in range(4):
                        dc, hl = h // 2, h % 2
                        pb = 64 * hl
                        ob = OB[h]
                        att = g_att[tt * 4 + h]

                        def ofn(e, ob=ob, h=h, dc=dc, hl=hl, tt=tt, att=att):
                            e.matmul(ps[ob][:, tt * 128:(tt + 1) * 128], lhsT=g_vb[:, tt, h * 128:(h + 1) * 128], rhs=att[:],
                                     start=True, stop=False, skip_group_check=True)
                            ins = None
                            for pp in range(2):
                                cs_ = tt * 128 + 64 * pp
                                ins = e.matmul(ps[ob][:, cs_:cs_ + 64],
                                               lhsT=g_S2b[:, dc, tt * 2 + pp, hl * 128:(hl + 1) * 128],
                                               rhs=g_qt[:, h, cs_:cs_ + 64],
                                               start=False, stop=(pp == 1), skip_group_check=True)
                            return ins

                        S.add("pe", ofn, reads=[("g_att", tt * 4 + h), ("g_vb", tt, h // 2), ("g_qt", h),
                                                ("S2b", dc, tt * 2), ("S2b", dc, tt * 2 + 1)],
                              writes=[("ps", ob)])
            for hp2 in range(2):
                hs = (2 * hp2, 2 * hp2 + 1)
                nb_ = {}
                for h in hs:
                    S.add("act", lambda e, ob=OB[h], sq=g_sq[h % 2]: e.activation(out=sq[:], in_=ps[ob][:], func=AF.Square),
                          reads=[("ps", OB[h])], writes=[("g_sq", h % 2)])
                for h in hs:
                    b = psum(MMB)
                    nb_[h] = b
                    S.add("pe", lambda e, b=b, sq=g_sq[h % 2]: e.matmul(ps[b][:], lhsT=ones_hd, rhs=sq[:], start=True, stop=True),
                          reads=[("g_sq", h % 2), "csts_b"], writes=[("ps", b)])
                for h in hs:
                    S.add("act", lambda e, b=nb_[h], rs=g_rs[h % 2]: e.activation(out=rs[:], in_=ps[b][:], func=AF.Ln, bias=EPS),
                          reads=[("ps", nb_[h])], writes=[("g_rs", h % 2)])
                for h in hs:
                    S.add("act", lambda e, rs=g_rs[h % 2]: e.activation(out=rs[:], in_=rs[:], func=AF.Exp, scale=-0.5),
                          reads=[("g_rs", h % 2)], writes=[("g_rs", h % 2)])
                for h in hs:
                    S.add("dve", lambda e, rs=g_rs[h % 2], wgt=g_wgt[h % 2], h=h: e.scalar_tensor_tensor(
                        out=wgt[:], in0=rs[:], scalar=0.5, in1=g_srb[:, h, :], op0=ALU.mult, op1=ALU.mult),
                        reads=[("g_rs", h % 2), ("g_srb", h)], writes=[("g_wgt", h % 2)])
                for h in hs:
                    S.add("dve", lambda e, ob=OB[h], wgt=g_wgt[h % 2], h=h, tb=tb: e.scalar_tensor_tensor(
                        out=y_bT[:, h, tb_sl(tb)], in0=ps[ob][:], scalar=vecs[:, V_GLA + 4 * l + h:V_GLA + 4 * l + h + 1], in1=wgt[:],
                        op0=ALU.mult, op1=ALU.mult),
                        reads=[("ps", OB[h]), ("g_wgt", h % 2), "vecs"], writes=[("yb", h, tb)])

    def att_phase(l):
        S.add("sp", lambda e: e.dma_start(out=a_mh[:], in_=mh_d[l, :, :].rearrange("p (h x) -> p h x", x=256)), writes=["a_mh"], dma=True)
        for i in range(2):
            S.add("dve", lambda e, i=i: e.memset(a_v[i][:, :, 0, 64:128], 1.0), writes=[(("a_v", i), g4, 0) for g4 in range(4)])
            S.add("dve", lambda e, i=i: e.memset(a_v[i][:, :, 1, 0:64], 1.0), writes=[(("a_v", i), g4, 1) for g4 in range(4)])
            S.add("dve", lambda e, i=i: e.memset(a_k[i][64:128, 0, :], 0.0), writes=[(("a_k", i), tb, 0) for tb in range(4)])
            S.add("dve", lambda e, i=i: e.memset(a_k[i][0:64, 1, :], 0.0), writes=[(("a_k", i), tb, 1) for tb in range(4)])
        MMB = [0, 1, 2]
        SB_ = [0, 1, 2, 3, 4]
        OBK = [5, 6, 7]
        def proj_tasks(hp):
            q, k, v = a_q[hp % 2], a_k[hp % 2], a_v[hp % 2]
            qk_, kk_, vk_ = ("a_q", hp % 2), ("a_k", hp % 2), ("a_v", hp % 2)
            st = {}
            tasks = []

            def load_qk():
                sl, wk = wload([(0, 8, 128, win_ap(l, C_QA + hp * 128, 128)), (1024, 8, 128, win_ap(l, C_KA + hp * 128, 128))])
                st["Wq"], st["Wk"], st["wk_qk"] = wview(sl, 8, 128, 0), wview(sl, 8, 128, 1024), wk
                sl, wk = wload([(0, 8, 128, win_ap(l, C_VA + hp * 128, 128))])
                st["Wv"], st["wk_v"] = wview(sl, 8, 128, 0), wk

            def qk_task(which, dst, dk, tb):
                def f():
                    if "Wq" not in st:
                        load_qk()
                    Wx = st[which]
                    b = psum(MMB)
                    mm_group(ps[b][:], [(Wx[:, kc, :], hT[:, kc, tb_sl(tb)]) for kc in range(8)],
                             reads=h_reads(tb) + [st["wk_qk"]], writes=[("ps", b)])
                    if which == "Wq":
                        S.add("dve", lambda e: e.tensor_copy(out=dst[:, tb_sl(tb)], in_=ps[b][:]),
                              reads=[("ps", b)], writes=[(dk, tb)])
                    else:
                        S.add("dve", lambda e: e.tensor_copy(out=dst[0:64, 0, tb_sl(tb)], in_=ps[b][0:64, :]),
                              reads=[("ps", b)], writes=[(dk, tb, 0)])
                        S.add("dve", lambda e: e.tensor_copy(out=dst[64:128, 1, tb_sl(tb)], in_=ps[b][64:128, :]),
                              reads=[("ps", b)], writes=[(dk, tb, 1)])
                return f

            def v_task(g4):
                def f():
                    Wv = st["Wv"]
                    b = psum(MMB)
                    for j in range(4):
                        gt = g4 * 4 + j
                        mm_group(ps[b][:, j * 128:(j + 1) * 128], [(hT[:, kc, gt * 128:(gt + 1) * 128], Wv[:, kc, :]) for kc in range(8)],
                                 reads=h_reads(g4) + [st["wk_v"]], writes=[("ps", b)])
                    pv = ps[b][:].rearrange("p (j c) -> p j c", c=128)
                    S.add("dve", lambda e: e.tensor_copy(out=v[:, g4 * 4:(g4 + 1) * 4, 0, 0:64], in_=pv[:, :, 0:64]),
                          reads=[("ps", b)], writes=[(vk_, g4, 0)])
                    S.add("dve", lambda e: e.tensor_copy(out=v[:, g4 * 4:(g4 + 1) * 4, 1, 64:128], in_=pv[:, :, 64:128]),
                          reads=[("ps", b)], writes=[(vk_, g4, 1)])
                return f

            for tb in range(4):
                tasks.append(qk_task("Wq", q, qk_, tb))
            for tb in range(4):
                tasks.append(qk_task("Wk", k, kk_, tb))
            for g4 in range(4):
                tasks.append(v_task(g4))
            tasks.append(load_qk)
            return tasks

        next_tasks = proj_tasks(0)
        next_tasks.pop()()
        for hp in range(4):
            q, k, v = a_q[hp % 2], a_k[hp % 2], a_v[hp % 2]
            qk_, kk_, vk_ = ("a_q", hp % 2), ("a_k", hp % 2), ("a_v", hp % 2)
            for f in next_tasks:
                f()
            next_tasks = proj_tasks(hp + 1) if hp + 1 < 4 else []
            if next_tasks:
                next_tasks.pop()()
            tiles = []
            for qb in range(4):
                c0 = 8 * qb
                for hd in range(2):
                    unit = []
                    for m in (4, 5, 6, 7, 3, 2, 1, 0):
                        j0 = c0 - 8 + 2 * m
                        if j0 < 0:
                            continue
                        lo, hi = max(0, 2 * m - 8), min(7, 2 * m + 1)
                        if m >= 4:
                            nd_hi, x0 = min(2 * m - 5, 7), 0
                            ndc = (nd_hi - lo + 1) * 64
                        elif m == 3:
                            ndc, x0 = 128, 128
                        else:
                            ndc, x0 = 0, 0
                        unit.append(dict(qb=qb, hd=hd, m=m, kt=j0 // 2, ncol=(hi - lo + 1) * 64, q0=(c0 + lo) * 64,
                                         oc=lo * 64, ndc=ndc, x0=x0, first=False, last=False))
                    unit[0]["first"] = True
                    unit[-1]["last"] = True
                    tiles.extend(unit)

            def emit_S(t):
                sb = psum(SB_)
                t["sb"] = sb
                pb = 64 * t["hd"]
                kt, q0, ncol = t["kt"], t["q0"], t["ncol"]
                hd_ = t["hd"]
                S.add("pe", lambda e, k=k, q=q: e.matmul(ps[sb][:, 0:ncol], lhsT=k[:, hd_, kt * 128:(kt + 1) * 128],
                                                         rhs=q[:, q0:q0 + ncol], start=True, stop=True),
                      reads=[(kk_, kt // 4, hd_), (qk_, t["qb"])], writes=[("ps", sb)])

            def emit_P(t):
                sb, ncol, ndc, x0 = t["sb"], t["ncol"], t["ndc"], t["x0"]
                h = 2 * hp + t["hd"]
                pi = nxt("a_pT", 4)
                pT = a_pT[pi]
                pk = ("a_pT", pi)
                t["pT"], t["pk"] = pT, pk
                if ndc > 0:
                    ti = nxt("a_t", 3)
                    tt_ = a_t[ti]
                    tk = ("a_t", ti)
                    S.add("dve", lambda e: e.scalar_tensor_tensor(
                        out=tt_[:, 0:ndc], in0=ps[sb][:, 0:ndc], scalar=0.125, in1=a_mh[:, h, x0:x0 + ndc],
                        op0=ALU.mult, op1=ALU.add),
                        reads=[("ps", sb), "a_mh"], writes=[tk, ("psx", sb)])
                    S.add("act", lambda e: e.activation(out=pT[:, 0:ndc], in_=tt_[:, 0:ndc], func=AF.Exp),
                          reads=[tk], writes=[pk])
                if ndc < ncol:
                    S.add("act", lambda e: e.activation(
                        out=pT[:, ndc:ncol], in_=ps[sb][:, ndc:ncol], func=AF.Exp, scale=0.125,
                        bias=vecs[:, V_CB + 8 * l + h:V_CB + 8 * l + h + 1]),
                        reads=[("ps", sb), "vecs", ("psx", sb)], writes=[pk])

            cur_ob = [None]

            def emit_PV(t):
                if t["first"]:
                    cur_ob[0] = psum(OBK)
                ob = cur_ob[0]
                t["ob"] = ob
                ncol, m, hd, kt, oc, pT = t["ncol"], t["m"], t["hd"], t["kt"], t["oc"], t["pT"]
                if m >= 4:
                    zr = (64, 128, 0, 64)
                else:
                    zr = (0, 64, ncol - 64, ncol)
                pk = t["pk"]
                S.add("pool", lambda e: e.memset(pT[zr[0]:zr[1], zr[2]:zr[3]], 0.0), writes=[pk])
                isfirst = t["first"]
                S.add("pe", lambda e, v=v: e.matmul(ps[ob][:, oc:oc + ncol], lhsT=v[:, kt, hd, :], rhs=pT[:, 0:ncol],
                                                    start=isfirst, stop=False, skip_group_check=True),
                      reads=[pk, (vk_, kt // 4, hd)], writes=[("ps", ob)])

            def emit_norm_act(t):
                ob, hd = t["ob"], t["hd"]
                pb = 64 * hd
                so = 64 - pb
                rci = nxt("a_rc", 2)
                rc = a_rc[rci]
                rk = ("a_rc", rci)
                t["rc"], t["rk"] = rc, rk
                S.add("act", lambda e: e.activation(out=rc[pb:pb + 64, :], in_=ps[ob][so:so + 64, :], func=AF.Ln),
                      reads=[("ps", ob)], writes=[rk])
                S.add("act", lambda e: e.activation(out=rc[pb:pb + 64, :], in_=rc[pb:pb + 64, :], func=AF.Exp, scale=-1.0),
                      reads=[rk], writes=[rk, ("psx", ob)])

            def emit_norm_dve(t):
                ob, hd, qb, rc, rk = t["ob"], t["hd"], t["qb"], t["rc"], t["rk"]
                pb = 64 * hd
                S.add("dve", lambda e, hp=hp: e.tensor_tensor(
                    out=y_aT[pb:pb + 64, hp, tb_sl(qb)], in0=ps[ob][pb:pb + 64, :], in1=rc[pb:pb + 64, :], op=ALU.mult),
                    reads=[("ps", ob), rk, ("psx", ob)], writes=[("ya", hp, qb, hd)])

            LA = 4
            D_ACT, D_DVE = 1, 6
            nt = len(tiles)
            pending = []
            for i in range(min(LA, nt)):
                emit_S(tiles[i])
            for i in range(nt):
                if i + LA < nt:
                    emit_S(tiles[i + LA])
                emit_P(tiles[i])
                emit_PV(tiles[i])
                pending = [(c + 1, t) for (c, t) in pending]
                for (c, t) in pending:
                    if c == D_ACT:
                        emit_norm_act(t)
                    if c == D_DVE:
                        emit_norm_dve(t)
                pending = [(c, t) for (c, t) in pending if c < D_DVE]
                if tiles[i]["last"]:
                    pending.append((0, tiles[i]))
            for (c, t) in pending:
                if c < D_ACT:
                    emit_norm_act(t)
                emit_norm_dve(t)

    def merge_phase(l, fuse_n2=True):
        S.add("dve", lambda e: e.memset(m_t[0][:, 0:8], 0.0), writes=["mrg_gate", ("m_t", 0)])
        wx_mode[0] = True
        for fcp in range(4):
            slb, wkb = wload([(0, 4, 256, w_branch[l, 0, :, fcp * 256:(fcp + 1) * 256].rearrange("(k p) c -> p k c", p=128)),
                              (1024, 4, 256, w_branch[l, 1, :, fcp * 256:(fcp + 1) * 256].rearrange("(k p) c -> p k c", p=128))])
            Wb = wview(slb, 8, 256)
            sla, wka = wload([(0, 8, 256, win_ap(l, C_GA + fcp * 256, 256))])
            Wa = wview(sla, 8, 256)
            slg, wkg = wload([(0, 8, 256, win_ap(l, C_GB + fcp * 256, 256))])
            Wg = wview(slg, 8, 256)
            for fi in range(2):
                fc = 2 * fcp + fi
                cs = slice(fi * 128, (fi + 1) * 128)
                for tb in range(4):
                    hr = h_reads(tb)
                    bga, bgb, bua, bub = psum(ALLB), psum(ALLB), psum(ALLB), psum(ALLB)
                    mm_group(ps[bga][:], [(Wa[:, kc, cs], hT[:, kc, tb_sl(tb)]) for kc in range(8)], reads=hr + [wka], writes=[("ps", bga)])
                    mm_group(ps[bgb][:], [(Wg[:, kc, cs], hT[:, kc, tb_sl(tb)]) for kc in range(8)], reads=hr + [wkg], writes=[("ps", bgb)])
                    mm_group(ps[bua][:], [(Wb[:, kc, cs], y_aT[:, kc, tb_sl(tb)]) for kc in range(4)],
                             reads=[("ya", kc, tb, hd) for kc in range(4) for hd in range(2)] + [wkb], writes=[("ps", bua)])
                    mm_group(ps[bub][:], [(Wb[:, 4 + kc, cs], y_bT[:, kc, tb_sl(tb)]) for kc in range(4)],
                             reads=[("yb", kc, tb) for kc in range(4)] + [wkb], writes=[("ps", bub)])
                    i0, i1 = nxt("m_th", 4), nxt("m_th", 4)
                    S.add("act", lambda e, b=bga, i0=i0: e.activation(out=m_th[i0][:], in_=ps[b][:], func=AF.Tanh, scale=0.5),
                          reads=[("ps", bga)], writes=[("m_th", i0)])
                    S.add("act", lambda e, b=bgb, i1=i1: e.activation(out=m_th[i1][:], in_=ps[b][:], func=AF.Tanh, scale=0.5),
                          reads=[("ps", bgb)], writes=[("m_th", i1)])
                    j0, j1 = nxt("m_t", 4), nxt("m_t", 4)
                    S.add("dve", lambda e, b=bua, i0=i0, j0=j0: e.scalar_tensor_tensor(out=m_t[j0][:], in0=m_th[i0][:], scalar=1.0, in1=ps[b][:],
                                                                                      op0=ALU.add, op1=ALU.mult),
                          reads=[("ps", bua), ("m_th", i0)], writes=[("m_t", j0)])
                    S.add("dve", lambda e, b=bub, i1=i1, j1=j1: e.scalar_tensor_tensor(out=m_t[j1][:], in0=m_th[i1][:], scalar=1.0, in1=ps[b][:],
                                                                                      op0=ALU.add, op1=ALU.mult),
                          reads=[("ps", bub), ("m_th", i1)], writes=[("m_t", j1)])
                    S.add("dve", lambda e, j0=j0, j1=j1, fc=fc, tb=tb: e.tensor_tensor(out=m_mg[:, fc, tb_sl(tb)], in0=m_t[j0][:], in1=m_t[j1][:], op=ALU.add),
                          reads=[("m_t", j0), ("m_t", j1)], writes=[("mg", fc, tb)])
        Wo = []
        for fcp in range(4):
            sl, wk = wload([(0, 8, 256, w_out[l, :, fcp * 256:(fcp + 1) * 256].rearrange("(k p) c -> p k c", p=128))])
            Wo.append((wview(sl, 8, 256), wk))

        def outproj(tb):
            for fc in range(8):
                W, wk = Wo[fc // 2]
                fi = fc % 2
                b = psum(ALLB)
                mm_group(ps[b][:], [(W[:, kc, fi * 128:(fi + 1) * 128], m_mg[:, kc, tb_sl(tb)]) for kc in range(8)],
                         reads=[("mg", kc, tb) for kc in range(8)] + [wk], writes=[("ps", b)])
                S.add("dve", lambda e, b=b, fc=fc, tb=tb: e.scalar_tensor_tensor(
                    out=resT[:, fc, tb_sl(tb)], in0=ps[b][:], scalar=0.5, in1=resT[:, fc, tb_sl(tb)], op0=ALU.mult, op1=ALU.add),
                    reads=[("ps", b), ("res", fc, tb)], writes=[("res", fc, tb)])

        n2t = (n2_sq, n2_ln, "n2")
        outproj(0)
        for tb in range(1, 4):
            outproj(tb)
            if fuse_n2:
                rmsnorm(V_MLP + 8 * l, h_dst, h_keys, tbs=(tb - 1,), temps=n2t)
        if fuse_n2:
            rmsnorm(V_MLP + 8 * l, h_dst, h_keys, tbs=(3,), temps=n2t)
        wx_mode[0] = False

    def mlp_phase(l):
        statq = []

        def emit_stat(item):
            qi, fc, tb = item
            S.add("pe", lambda e: e.matmul(ps[4 + tb][:], lhsT=ones_mean, rhs=l_sq[qi][:], start=(fc == 0), stop=(fc == 7),
                                           skip_group_check=True),
                  reads=[("l_sq", qi), "csts_b"], writes=[("ps", 4 + tb)])

        for half in range(2):
            for j in range(8):
                sl, wk = wload([(0, 8, 256, w_up[l, :, half * 2048 + j * 256: half * 2048 + (j + 1) * 256].rearrange("(k p) c -> p k c", p=128))])
                W = wview(sl, 8, 256)
                for fi in range(2):
                    ffc = 2 * j + fi
                    for tb in range(4):
                        b = psum(ALLB)
                        mm_group(ps[b][:], [(W[:, kc, fi * 128:(fi + 1) * 128], hT[:, kc, tb_sl(tb)]) for kc in range(8)],
                                 reads=h_reads(tb) + [wk], writes=[("ps", b)])
                        ri = nxt("l_r", 3)
                        S.add("act", lambda e, b=b, ri=ri: e.activation(out=l_r[ri][:], in_=ps[b][:], func=AF.Relu),
                              reads=[("ps", b)], writes=[("l_r", ri)])
                        S.add("dve", lambda e, ri=ri, ffc=ffc, tb=tb: e.tensor_tensor(out=l_a[:, ffc, tb_sl(tb)], in0=l_r[ri][:], in1=l_r[ri][:], op=ALU.mult),
                              reads=[("l_r", ri)], writes=[("la", ffc, tb)])
            for fc in range(8):
                sl, wk = wload([(0, 16, 128, w_down[l, half * 2048:(half + 1) * 2048, fc * 128:(fc + 1) * 128].rearrange("(k p) c -> p k c", p=128))])
                W = wview(sl, 16, 128)
                for tb in range(4):
                    b = psum(ALLB if half == 0 else [0, 1, 2, 3])
                    mm_group(ps[b][:], [(W[:, kc, :], l_a[:, kc, tb_sl(tb)]) for kc in range(16)],
                             reads=[("la", kc, tb) for kc in range(16)] + [wk], writes=[("ps", b)])
                    S.add("dve", lambda e, b=b, fc=fc, tb=tb: e.tensor_tensor(out=resT[:, fc, tb_sl(tb)], in0=ps[b][:], in1=resT[:, fc, tb_sl(tb)], op=ALU.add),
                          reads=[("ps", b), ("res", fc, tb)], writes=[("res", fc, tb)])
                    if half == 1:
                        qi = nxt("l_sq", 6)
                        S.add("act", lambda e, qi=qi, fc=fc, tb=tb: e.activation(out=l_sq[qi][:], in_=resT[:, fc, tb_sl(tb)], func=AF.Square),
                              reads=[("res", fc, tb)], writes=[("l_sq", qi)])
                        statq.append((qi, fc, tb))
                        if len(statq) > 4:
                            emit_stat(statq.pop(0))
            if half == 1:
                while statq:
                    emit_stat(statq.pop(0))


    out_ops = []

    def dump(src, nchunks, is_bf16):
        S.barrier(include_pool=True)
        for c in range(nchunks):
            out_ops.append(S.add("pool", lambda e, c=c: e.dma_start(out=outT[c * 128:(c + 1) * 128, :], in_=src[:, c, :]), dma=True))

    done = False
    for l in range(L):
        S.epoch = l
        if l == 0:
            rmsnorm(V_MIX + 8 * l, h_dst, h_keys)
        else:
            norm_tail(V_MIX + 8 * l, h_out)
        if tap == ("n1", l):
            dump(hT, 8, True); done = True; break
        S.barrier()
        gla_phase(l)
        if tap == ("gla", l):
            dump(y_bT, 4, True); done = True; break
        S.barrier()
        att_phase(l)
        if tap == ("att", l):
            dump(y_aT, 4, True); done = True; break
        S.barrier()
        merge_phase(l)
        if tap == ("mix", l):
            dump(resT, 8, False); done = True; break
        mlp_phase(l)
        if tap == ("layer", l):
            dump(resT, 8, False); done = True; break
    if not done:
        S.barrier()
        if final_norm:
            def f_out(c, tb, ln, lnk, gcol):
                fi = nxt("f_o", 4)
                S.add("dve", lambda e: e.scalar_tensor_tensor(
                    out=f_o[fi][:], in0=resT[:, c, tb_sl(tb)], scalar=vecs[:, gcol + c:gcol + c + 1], in1=ln[:],
                    op0=ALU.mult, op1=ALU.mult),
                    reads=[("res", c, tb), lnk, "vecs"], writes=[("f_o", fi)])
                out_ops.append(S.add("sp", lambda e: e.dma_start(out=outT[c * 128:(c + 1) * 128, tb_sl(tb)], in_=f_o[fi][:]),
                                     reads=[("f_o", fi)], dma=True))

            norm_tail(V_FIN, f_out)
        else:
            for c in range(8):
                out_ops.append(S.add("sp", lambda e, c=c: e.dma_start(out=outT[c * 128:(c + 1) * 128, :], in_=resT[:, c, :]),
                                     reads=[("res", c, tb) for tb in range(4)], dma=True))
    S.emit(final_wait_ops=out_ops)
    return nc


def _consts():
    t = np.arange(128)
    same = (t[:, None] // 64) == (t[None, :] // 64)
    U = (same & (t[:, None] <= t[None, :])).astype(np.float32)
    R = (same & (t[:, None] > t[None, :])).astype(np.float32)
    M = (same & (t[None, :] >= t[:, None])).astype(np.float32)
    c = np.zeros((128, 5 * 128), np.float32)
    c[:, 0:128] = 1.0 / 1024
    c[:, 128:256] = 1.0 / 128
    c[:, 256:384] = U
    c[:, 384:512] = R
    c[:, 512:640] = M
    return c


def _prep_shared(inp, layers):
    L = len(layers)
    f32 = lambda a: np.ascontiguousarray(np.asarray(a, dtype=np.float32))
    sh = {}
    sh["w_in"] = f32(inp["w_in"][layers])
    sh["w_branch"] = f32(inp["w_branch"][layers])
    sh["w_out"] = f32(inp["w_out"][layers])
    sh["w_up"] = f32(inp["w_up"][layers])
    sh["w_down"] = f32(inp["w_down"][layers])
    sh["wg"] = f32(inp["w_gate_lr"][layers])
    sh["bg"] = f32(np.broadcast_to(np.asarray(inp["b_gate"])[layers][:, None, :], (L, 128, 256)))
    rb = np.asarray(inp["rel_bias"], dtype=np.float32)[layers]
    kk = np.arange(128)[:, None]
    xx = np.arange(256)[None, :]
    idx = np.clip(xx - kk, -128, 128) + 128
    mh = rb[:, :, idx]
    sh["mh"] = f32(np.transpose(mh, (0, 2, 1, 3)).reshape(L, 128, 8 * 256))
    vec = np.zeros((128, 128), np.float32)
    mg = np.asarray(inp["mix_norm_g"], dtype=np.float32)[layers]
    lg = np.asarray(inp["mlp_norm_g"], dtype=np.float32)[layers]
    gg = np.asarray(inp["gla_norm_g"], dtype=np.float32)[layers]
    for i in range(L):
        vec[:, 8 * i:8 * i + 8] = mg[i].reshape(8, 128).T
        vec[:, 32 + 8 * i:32 + 8 * i + 8] = lg[i].reshape(8, 128).T
        vec[:, 72 + 4 * i:72 + 4 * i + 4] = gg[i].reshape(4, 128).T
        vec[:, 88 + 8 * i:88 + 8 * i + 8] = rb[i, :, 256][None, :]
    vec[:, 64:72] = np.asarray(inp["final_norm_g"], dtype=np.float32).reshape(8, 128).T
    sh["vecs"] = vec
    sh["csts"] = _consts()
    return sh


_PROGS = {}


def _prog(key, **kw):
    if key not in _PROGS:
        _PROGS[key] = build(**kw)
    return _PROGS[key]


def kernel(**inp):
    x = np.asarray(inp["x"], dtype=np.float32)
    xTs = [np.ascontiguousarray(x[b].T) for b in range(NCORES)]
    sh = _prep_shared(inp, list(range(DEPTH)))
    nc = _prog("full", n_layers=DEPTH)
    in_maps = [dict(sh, xT=xTs[b]) for b in range(NCORES)]
    res = run_bass_kernel_spmd(nc, in_maps, core_ids=list(range(NCORES)))
    out = np.stack([np.ascontiguousarray(res.results[b]["outT"].T) for b in range(NCORES)], axis=0)
    return out.astype(np.float32)
```

```python
import contextlib
import numpy as np
import concourse.bass as bass
import concourse.mybir as mybir
from concourse.bass_utils import run_bass_kernel_spmd

F32 = mybir.dt.float32
BF16 = mybir.dt.bfloat16
AF = mybir.ActivationFunctionType
ALU = mybir.AluOpType

D = 1024
SEQ = 2048
DEPTH = 4
NCORES = 8
IN_COLS = 5136
EPS = 1e-6
C_QA, C_KA, C_VA, C_QB, C_KB, C_VB, C_RB, C_LR, C_GA, C_GB = 0, 512, 1024, 1536, 1792, 2048, 2560, 3072, 3088, 4112


class Op:
    __slots__ = ("eng", "fn", "deps", "is_dma", "signals", "sem", "val", "epoch", "prev_dma")

    def __init__(self, eng, fn, is_dma, epoch):
        self.eng = eng
        self.fn = fn
        self.is_dma = is_dma
        self.deps = []
        self.signals = False
        self.sem = None
        self.val = 0
        self.epoch = epoch
        self.prev_dma = None


class Sched:
    N_DMA_SEMS = 8

    def __init__(self, nc):
        self.nc = nc
        self.engs = {"pe": nc.tensor, "act": nc.scalar, "dve": nc.vector, "pool": nc.gpsimd, "sp": nc.sync}
        self.ops = {e: [] for e in self.engs}
        self.last_writer = {}
        self.readers = {}
        self.epoch = 0
        self.pending_barrier = {}

    def barrier(self, include_pool=False):
        last = [lst[-1] for e, lst in self.ops.items() if lst and e in ("pe", "act", "dve")]
        for e in self.engs:
            if e == "pool" and not include_pool:
                continue
            if e == "pe":
                continue
            self.pending_barrier[e] = last

    def add(self, eng, fn, reads=(), writes=(), dma=False):
        op = Op(eng, fn, dma, self.epoch)
        deps = {}

        def add_dep(d, raw):
            if d is op:
                return
            if (not d.is_dma) and d.eng == eng and not dma:
                if eng == "pe":
                    return
            deps[id(d)] = d

        for k in reads:
            w = self.last_writer.get(k)
            if w is not None:
                add_dep(w, True)
        for k in writes:
            w = self.last_writer.get(k)
            if w is not None:
                add_dep(w, False)
            rd = self.readers.get(k)
            if rd:
                for r in rd.values():
                    add_dep(r, False)
        pb = self.pending_barrier.pop(eng, None)
        if pb:
            for d in pb:
                if d.eng != eng or dma:
                    deps[id(d)] = d
        op.deps = list(deps.values())
        for k in reads:
            rd = self.readers.setdefault(k, {})
            if dma:
                rd[("dma", id(op))] = op
            else:
                rd[eng] = op
        for k in writes:
            self.last_writer[k] = op
            self.readers[k] = {}
        self.ops[eng].append(op)
        return op

    def emit(self, final_wait_ops=()):
        nc = self.nc
        for e, lst in self.ops.items():
            for op in lst:
                for d in op.deps:
                    d.signals = True
        n_epochs = self.epoch + 1
        with contextlib.ExitStack() as st:
            csem = {}
            for e in self.engs:
                for ep in range(n_epochs):
                    csem[(e, ep)] = st.enter_context(nc.semaphore(f"c_{e}_{ep}"))
            dsem = {}
            for e in self.engs:
                if any(op.is_dma for op in self.ops[e]):
                    dsem[e] = [st.enter_context(nc.semaphore(f"d_{e}_{i}")) for i in range(self.N_DMA_SEMS)]
            for e, lst in self.ops.items():
                cnt = {}
                dcnt = [0] * self.N_DMA_SEMS
                dlast = [None] * self.N_DMA_SEMS
                nd = 0
                for op in lst:
                    if op.is_dma:
                        k = nd % self.N_DMA_SEMS
                        nd += 1
                        dcnt[k] += 16
                        op.sem = dsem[e][k]
                        op.val = dcnt[k]
                        op.prev_dma = dlast[k]
                        dlast[k] = op
                    elif op.signals:
                        cnt[op.epoch] = cnt.get(op.epoch, 0) + 1
                        op.sem = csem[(e, op.epoch)]
                        op.val = cnt[op.epoch]
            block = st.enter_context(nc.Block())
            for e in self.engs:
                self._emit_engine(block, e, final_wait_ops if e == "sp" else ())

    def _emit_engine(self, block, e, final_wait_ops):
        lst = self.ops[e]
        if not lst and not final_wait_ops:
            return
        reg = {"pe": block.tensor, "act": block.scalar, "dve": block.vector, "pool": block.gpsimd, "sp": block.sync}[e]

        @reg
        def _(eng):
            waited = {}

            def wait(sem, val):
                key = id(sem)
                if waited.get(key, 0) < val:
                    eng.wait_ge(sem, val)
                    waited[key] = val

            for op in lst:
                for d in op.deps:
                    wait(d.sem, d.val)
                if op.is_dma and op.prev_dma is not None:
                    wait(op.prev_dma.sem, op.prev_dma.val)
                ins = op.fn(eng)
                if op.is_dma:
                    ins.then_inc(op.sem, 16)
                elif op.signals:
                    ins.then_inc(op.sem, 1)
            for d in final_wait_ops:
                wait(d.sem, d.val)


def build(n_layers=DEPTH, final_norm=True, tap=None):
    nc = bass.Bass("TRN2", target_bir_lowering=False)
    L = n_layers

    def din(name, shape):
        return nc.dram_tensor(name, shape, F32, kind="ExternalInput").ap()

    xT = din("xT", [D, SEQ])
    w_in = din("w_in", [L, D, IN_COLS])
    w_branch = din("w_branch", [L, 2, 512, D])
    w_out = din("w_out", [L, D, D])
    w_up = din("w_up", [L, D, 4 * D])
    w_down = din("w_down", [L, 4 * D, D])
    wg_d = din("wg", [L, 16, 256])
    bg_d = din("bg", [L, 128, 256])
    mh_d = din("mh", [L, 128, 8 * 256])
    vec_d = din("vecs", [128, 128])
    cst_d = din("csts", [128, 5 * 128])
    if tap is None:
        outT = nc.dram_tensor("outT", [D, SEQ], F32, kind="ExternalOutput").ap()
    else:
        outT = nc.dram_tensor("outT", [D, SEQ], F32, kind="ExternalOutput").ap()

    V_MIX, V_MLP, V_FIN, V_GLA, V_CB = 0, 32, 64, 72, 88

    base = (nc.sbuf_base + 63) // 64 * 64
    top = nc.sbuf_top
    cur = [base]

    def alloc(name, shape, dt, at=None):
        sz = int(np.prod(shape[1:])) * (4 if dt == F32 else 2)
        sz = (sz + 63) // 64 * 64
        if at is None:
            o = cur[0]
            cur[0] += sz
        else:
            o = at
        assert o + sz <= top, (name, o, sz, top)
        return nc.alloc_sbuf_tensor_at(name, shape, dt, offset=o), o + sz

    resT, _ = alloc("resT", [128, 8, SEQ], F32)
    hT, _ = alloc("hT", [128, 8, SEQ], BF16)
    NB = 4
    wbf, _ = alloc("wbf", [128, NB, 2048], BF16)
    vecs, _ = alloc("vecs", [128, 128], F32)
    csts, _ = alloc("csts", [128, 5 * 128], F32)
    csts_b, _ = alloc("csts_b", [128, 5 * 128], BF16)
    lrbW, _ = alloc("lrbW", [128, 8, 16], BF16)
    scratch = cur[0]
    SCR = top - scratch
    assert SCR >= 86000, SCR

    class Region:
        def __init__(self, start):
            self.o = start

        def a(self, name, shape, dt):
            t, e = alloc(name, shape, dt, at=self.o)
            self.o = e
            return t

    y_bT, _ = alloc("y_bT", [128, 4, SEQ], BF16, at=scratch)
    y_aT, _ = alloc("y_aT", [128, 4, SEQ], BF16, at=scratch + 16384)
    R2 = scratch + 32768
    rn = Region(top - 20480 - 64)
    n_sq = [rn.a(f"n_sq{i}", [128, 8, 512], BF16) for i in range(2)]
    n_ln = [rn.a(f"n_ln{i}", [128, 512], F32) for i in range(2)]
    rn2 = Region(scratch + 12288)
    n2_sq = [rn2.a(f"n2_sq{i}", [128, 8, 512], BF16) for i in range(2)]
    n2_ln = [rn2.a(f"n2_ln{i}", [128, 512], F32) for i in range(2)]
    assert rn2.o <= scratch + 32768
    rg = Region(scratch + 16384)
    g_lrbT = [rg.a(f"g_lrbT{i}", [128, 512], F32) for i in range(2)]
    g_la = rg.a("g_la", [128, 4, 256], F32)
    g_z = [rg.a(f"g_z{i}", [128, 256], F32) for i in range(4)]
    g_eb = rg.a("g_eb", [128, 2, 512], F32)
    g_enb = rg.a("g_enb", [128, 2, 512], F32)
    g_eend = rg.a("g_eend", [128, 4, 256], F32)
    g_dec = rg.a("g_dec", [128, 2, 32], F32)
    g_qt = rg.a("g_qt", [128, 4, 512], BF16)
    g_kt = rg.a("g_kt", [128, 2, 512], BF16)
    g_kend = rg.a("g_kend", [128, 4, 2, 256], BF16)
    g_vb = rg.a("g_vb", [128, 4, 512], BF16)
    g_th = [rg.a(f"g_th{i}", [128, 512], F32) for i in range(2)]
    g_srb = rg.a("g_srb", [128, 4, 512], BF16)
    g_S2 = rg.a("g_S2", [128, 2, 2, 256], F32)
    g_S2b = rg.a("g_S2b", [128, 2, 8, 256], BF16)
    g_att = [rg.a(f"g_att{i}", [128, 128], BF16) for i in range(16)]
    g_sq = [rg.a(f"g_sq{i}", [128, 512], BF16) for i in range(2)]
    g_rs = [rg.a(f"g_rs{i}", [128, 512], F32) for i in range(2)]
    g_wgt = [rg.a(f"g_wgt{i}", [128, 512], F32) for i in range(2)]
    g_bg = rg.a("g_bg", [128, 256], F32)
    g_wg = rg.a("g_wg", [128, 256], F32)
    assert rg.o <= top
    ra = Region(R2)
    a_q = [ra.a(f"a_q{i}", [128, SEQ], BF16) for i in range(2)]
    a_k = [ra.a(f"a_k{i}", [128, 2, SEQ], BF16) for i in range(2)]
    a_v = [ra.a(f"a_v{i}", [128, 16, 2, 128], BF16) for i in range(2)]
    a_mh = ra.a("a_mh", [128, 8, 256], F32)
    a_pT = [ra.a(f"a_pT{i}", [128, 512], BF16) for i in range(4)]
    a_t = [ra.a(f"a_t{i}", [128, 256], F32) for i in range(3)]
    a_rc = [ra.a(f"a_rc{i}", [128, 512], F32) for i in range(2)]
    assert ra.o <= top
    rm = Region(R2)
    m_mg = rm.a("m_mg", [128, 8, SEQ], BF16)
    m_th = [rm.a(f"m_th{i}", [128, 512], F32) for i in range(4)]
    m_t = [rm.a(f"m_t{i}", [128, 512], F32) for i in range(4)]
    NBX = 2
    m_wx = rm.a("m_wx", [128, NBX, 2048], BF16)
    assert rm.o <= top
    rl = Region(scratch)
    l_a = rl.a("l_a", [128, 16, SEQ], BF16)
    l_r = [rl.a(f"l_r{i}", [128, 512], F32) for i in range(3)]
    l_sq = [rl.a(f"l_sq{i}", [128, 512], BF16) for i in range(6)]
    assert rl.o <= top
    rf = Region(R2)
    f_o = [rf.a(f"f_o{i}", [128, 512], F32) for i in range(4)]

    ps = [nc.alloc_psum_tensor(f"ps{i}", [128, 512], F32) for i in range(8)]

    S = Sched(nc)

    rot = {}

    def nxt(name, n):
        i = rot.get(name, 0)
        rot[name] = i + 1
        return i % n

    def psum(pool):
        key = tuple(pool)
        return pool[nxt(("ps", key), len(pool))]

    ALLB = list(range(8))

    wx_mode = [False]

    def wslot_ap(slot):
        if slot >= NB:
            return m_wx[:, slot - NB, :]
        return wbf[:, slot, :]

    def wload(parts):
        if wx_mode[0]:
            slot = [0, 1, 2, 3, NB, NB + 1][nxt("wslotx", NB + NBX)] if NB == 4 else nxt("wslot", NB)
        else:
            slot = nxt("wslot", NB)
        key = ("w", slot)
        rd = ["mrg_gate"] if slot >= NB else []
        for (o, nk, ncol, ap) in parts:
            dst = wslot_ap(slot)[:, o:o + nk * ncol].rearrange("p (k c) -> p k c", c=ncol)
            S.add("pool", lambda e, dst=dst, ap=ap: e.dma_start(out=dst, in_=ap), reads=rd, writes=[key], dma=True)
        return slot, key

    def wview(slot, nk, ncol, o=0):
        return wslot_ap(slot)[:, o:o + nk * ncol].rearrange("p (k c) -> p k c", c=ncol)

    def win_ap(l, c0, ncol):
        return w_in[l, :, c0:c0 + ncol].rearrange("(k p) c -> p k c", p=128)

    def mm_group(out_ap, pairs, reads, writes):
        n = len(pairs)

        def fn(e):
            ins = None
            for i, (lt, rh) in enumerate(pairs):
                ins = e.matmul(out_ap, lhsT=lt, rhs=rh, start=(i == 0), stop=(i == n - 1))
            return ins

        return S.add("pe", fn, reads=reads, writes=writes)

    def tb_sl(tb):
        return slice(tb * 512, (tb + 1) * 512)

    S.add("sp", lambda e: e.dma_start(out=vecs[:], in_=vec_d[:, :]), writes=["vecs"], dma=True)
    S.add("sp", lambda e: e.dma_start(out=csts[:], in_=cst_d[:, :]), writes=["csts"], dma=True)
    S.add("dve", lambda e: e.tensor_copy(out=csts_b[:], in_=csts[:]), reads=["csts"], writes=["csts_b"])
    ones_mean = csts_b[:, 0:128]
    ones_hd = csts_b[:, 128:256]
    U_f = csts[:, 256:384]
    R_f = csts[:, 384:512]
    maskT = csts[:, 512:640]
    xv = xT.rearrange("(c p) t -> p c t", p=128)
    for tb in range(4):
        S.add("sp", lambda e, tb=tb: e.dma_start(out=resT[:, :, tb_sl(tb)], in_=xv[:, :, tb_sl(tb)]),
              writes=[("res", c, tb) for c in range(8)], dma=True)

    def rmsnorm(gcol, dst_fn, dst_keys_fn, tbs=(0, 1, 2, 3), temps=None):
        t_sq, t_ln, t_name = temps if temps is not None else (n_sq, n_ln, "n")
        for tb in tbs:
            sq = t_sq[tb % 2]
            sqk = (t_name + "_sq", tb % 2)
            for c in range(8):
                S.add("act", lambda e, c=c, tb=tb, sq=sq: e.activation(out=sq[:, c, :], in_=resT[:, c, tb_sl(tb)], func=AF.Square),
                      reads=[("res", c, tb)], writes=[(sqk, c)])
            b = psum(ALLB)
            mm_group(ps[b][:], [(ones_mean, sq[:, c, :]) for c in range(8)],
                     reads=[(sqk, c) for c in range(8)] + ["csts_b"], writes=[("ps", b)])
            ln = t_ln[tb % 2]
            lnk = (t_name + "_ln", tb % 2)
            S.add("act", lambda e, b=b, ln=ln: e.activation(out=ln[:], in_=ps[b][:], func=AF.Ln, bias=EPS),
                  reads=[("ps", b)], writes=[lnk])
            S.add("act", lambda e, ln=ln: e.activation(out=ln[:], in_=ln[:], func=AF.Exp, scale=-0.5),
                  reads=[lnk], writes=[lnk])
            for c in range(8):
                S.add("dve", lambda e, c=c, tb=tb, ln=ln: e.scalar_tensor_tensor(
                    out=dst_fn(c, tb), in0=resT[:, c, tb_sl(tb)], scalar=vecs[:, gcol + c:gcol + c + 1], in1=ln[:],
                    op0=ALU.mult, op1=ALU.mult),
                    reads=[("res", c, tb), lnk, "vecs"], writes=dst_keys_fn(c, tb))

    def norm_tail(gcol, emit_out):
        for tb in range(4):
            b = 4 + tb
            ln = n_ln[tb % 2]
            lnk = ("n_ln", tb % 2)
            S.add("act", lambda e, b=b, ln=ln: e.activation(out=ln[:], in_=ps[b][:], func=AF.Ln, bias=EPS),
                  reads=[("ps", b)], writes=[lnk])
            S.add("act", lambda e, ln=ln: e.activation(out=ln[:], in_=ln[:], func=AF.Exp, scale=-0.5),
                  reads=[lnk], writes=[lnk])
            for c in range(8):
                emit_out(c, tb, ln, lnk, gcol)

    def h_out(c, tb, ln, lnk, gcol):
        S.add("dve", lambda e: e.scalar_tensor_tensor(
            out=hT[:, c, tb_sl(tb)], in0=resT[:, c, tb_sl(tb)], scalar=vecs[:, gcol + c:gcol + c + 1], in1=ln[:],
            op0=ALU.mult, op1=ALU.mult),
            reads=[("res", c, tb), lnk, "vecs"], writes=[("h", c, tb)])

    def h_dst(c, tb):
        return hT[:, c, tb_sl(tb)]

    def h_keys(c, tb):
        return [("h", c, tb)]

    def h_reads(tb):
        return [("h", c, tb) for c in range(8)]

    def gla_phase(l):
        S.add("sp", lambda e: e.dma_start(out=g_bg[:], in_=bg_d[l, :, :]), writes=["g_bg"], dma=True)
        S.add("sp", lambda e: e.dma_start(out=g_wg[0:16, :], in_=wg_d[l, :, :]), writes=["g_wg"], dma=True)
        S.add("pool", lambda e: e.dma_start(out=lrbW[:], in_=win_ap(l, C_LR, 16)), writes=["lrbW"], dma=True)
        S.add("dve", lambda e: e.memset(g_S2[:], 0.0), writes=[("S2", 0, 0), ("S2", 0, 1), ("S2", 1, 0), ("S2", 1, 1)])
        for h in range(4):
            zr0 = 64 * (1 - h % 2)
            S.add("dve", lambda e, h=h, zr0=zr0: e.memset(g_qt[zr0:zr0 + 64, h, :], 0.0), writes=[("g_qt", h)])
        for p in range(2):
            zr0 = 64 * (1 - p)
            S.add("dve", lambda e, p=p, zr0=zr0: e.memset(g_kend[zr0:zr0 + 64, :, p, :], 0.0), writes=[("g_kend", tt, p) for tt in range(4)])
        MMB = [0, 1, 2, 3]
        INTRA_AHEAD = 1
        for tb in range(4):
            hr = h_reads(tb)
            b = psum(MMB)
            mm_group(ps[b][0:16, :], [(lrbW[:, kc, :], hT[:, kc, tb_sl(tb)]) for kc in range(8)],
                     reads=hr + ["lrbW"], writes=[("ps", b)])
            lrbT = g_lrbT[tb % 2]
            lk = ("g_lrbT", tb % 2)
            S.add("act", lambda e, b=b, lrbT=lrbT: e.activation(out=lrbT[0:16, :], in_=ps[b][0:16, :], func=AF.Copy),
                  reads=[("ps", b)], writes=[lk])
            sl, wk = wload([(0, 8, 256, win_ap(l, C_VB + 0 * 256, 256))])
            W = wview(sl, 8, 256)
            for tt in range(4):
                b = psum(MMB)
                t0 = tb * 512 + tt * 128
                mm_group(ps[b][:, 0:256], [(hT[:, kc, t0:t0 + 128], W[:, kc, :]) for kc in range(8)],
                         reads=hr + [wk], writes=[("ps", b)])
                S.add("act", lambda e, b=b, tt=tt, u=0: e.activation(out=g_vb[:, tt, u * 256:(u + 1) * 256], in_=ps[b][:, 0:256], func=AF.Copy),
                      reads=[("ps", b)], writes=[("g_vb", tt, 0)])
            bB = [4, 5]
            zb = []
            for tt in range(4):
                b = 6 + tt // 2
                zb.append(b)
                S.add("pe", lambda e, b=b, tt=tt, lrbT=lrbT: e.matmul(ps[b][:, (tt % 2) * 256:(tt % 2) * 256 + 256], lhsT=lrbT[0:16, tt * 128:(tt + 1) * 128],
                                                                      rhs=g_wg[0:16, :], start=True, stop=True),
                      reads=[lk, "g_wg"], writes=[("ps", b)])
            for tt in range(4):
                S.add("dve", lambda e, b=zb[tt], tt=tt: e.tensor_tensor(out=g_z[tt][:], in0=ps[b][:, (tt % 2) * 256:(tt % 2) * 256 + 256], in1=g_bg[:], op=ALU.add),
                      reads=[("ps", zb[tt]), "g_bg"], writes=[("g_z", tt)])
            for tt in range(4):
                S.add("act", lambda e, tt=tt: e.activation(out=g_z[tt][:], in_=g_z[tt][:], func=AF.Exp, scale=-1.0),
                      reads=[("g_z", tt)], writes=[("g_z", tt)])
            for tt in range(4):
                S.add("act", lambda e, tt=tt: e.activation(out=g_la[:, tt, :], in_=g_z[tt][:], func=AF.Ln, bias=1.0),
                      reads=[("g_z", tt)], writes=[("g_la", tt)])
            sl, wk = wload([(0, 8, 256, win_ap(l, C_VB + 1 * 256, 256))])
            W = wview(sl, 8, 256)
            for tt in range(4):
                b = psum(MMB)
                t0 = tb * 512 + tt * 128
                mm_group(ps[b][:, 0:256], [(hT[:, kc, t0:t0 + 128], W[:, kc, :]) for kc in range(8)],
                         reads=hr + [wk], writes=[("ps", b)])
                S.add("act", lambda e, b=b, tt=tt, u=1: e.activation(out=g_vb[:, tt, u * 256:(u + 1) * 256], in_=ps[b][:, 0:256], func=AF.Copy),
                      reads=[("ps", b)], writes=[("g_vb", tt, 1)])
            rb_ = []
            for tt in range(4):
                lak = ("g_la", tt)
                for dc in range(2):
                    S.add("pe", lambda e, dc=dc, tt=tt: e.matmul(ps[bB[dc]][:, tt * 128:(tt + 1) * 128],
                                                                 lhsT=g_la[:, tt, dc * 128:(dc + 1) * 128], rhs=U_f,
                                                                 start=True, stop=True),
                          reads=[lak, "csts"], writes=[("ps", bB[dc])])
                b2 = 6 + tt // 2
                rb_.append(b2)
                S.add("pe", lambda e, b2=b2, tt=tt: e.matmul(ps[b2][:, (tt % 2) * 256:(tt % 2) * 256 + 256], lhsT=R_f, rhs=g_la[:, tt, :], start=True, stop=True),
                      reads=[lak, "csts"], writes=[("ps", b2)])
            for u in range(2):
                sl, wk = wload([(0, 8, 256, win_ap(l, C_RB + u * 256, 256))])
                W = wview(sl, 8, 256)
                for hh in range(2):
                    h = 2 * u + hh
                    b = psum(MMB)
                    mm_group(ps[b][:], [(W[:, kc, hh * 128:(hh + 1) * 128], hT[:, kc, tb_sl(tb)]) for kc in range(8)],
                             reads=hr + [wk], writes=[("ps", b)])
                    th = g_th[h % 2]
                    thk = ("g_th", h % 2)
                    S.add("act", lambda e, b=b, th=th: e.activation(out=th[:], in_=ps[b][:], func=AF.Tanh, scale=0.5),
                          reads=[("ps", b)], writes=[thk])
                    S.add("dve", lambda e, b=b, th=th, h=h: e.scalar_tensor_tensor(out=g_srb[:, h, :], in0=th[:], scalar=1.0, in1=ps[b][:],
                                                                                  op0=ALU.add, op1=ALU.mult),
                          reads=[("ps", b), thk], writes=[("g_srb", h)])
            for tt in range(4):
                S.add("act", lambda e, b2=rb_[tt], tt=tt: e.activation(out=g_eend[:, tt, :], in_=ps[b2][:, (tt % 2) * 256:(tt % 2) * 256 + 256], func=AF.Exp, scale=-1.0 / 16),
                      reads=[("ps", rb_[tt])], writes=[("g_eend", tt)])
            for dc in range(2):
                S.add("act", lambda e, dc=dc: e.activation(out=g_eb[:, dc, :], in_=ps[bB[dc]][:], func=AF.Exp, scale=-1.0 / 16),
                      reads=[("ps", bB[dc])], writes=[("g_eb", dc)])
                S.add("act", lambda e, dc=dc: e.activation(out=g_enb[:, dc, :], in_=ps[bB[dc]][:], func=AF.Exp, scale=1.0 / 16),
                      reads=[("ps", bB[dc])], writes=[("g_enb", dc)])
                S.add("act", lambda e, dc=dc, tb=tb: e.activation(out=g_dec[:, dc, tb * 8:(tb + 1) * 8], in_=ps[bB[dc]][:, 63::64],
                                                                  func=AF.Exp, scale=-1.0 / 16),
                      reads=[("ps", bB[dc])], writes=[("g_dec", dc)])
            sl, wk = wload([(0, 8, 256, win_ap(l, C_QB, 256))])
            W = wview(sl, 8, 256)
            for dc in range(2):
                b = psum(MMB)
                mm_group(ps[b][:], [(W[:, kc, dc * 128:(dc + 1) * 128], hT[:, kc, tb_sl(tb)]) for kc in range(8)],
                         reads=hr + [wk], writes=[("ps", b)])
                for hl in range(2):
                    S.add("dve", lambda e, b=b, dc=dc, hl=hl: e.scalar_tensor_tensor(
                        out=g_qt[64 * hl:64 * hl + 64, 2 * dc + hl, :], in0=ps[b][64 * hl:64 * hl + 64, :], scalar=0.125,
                        in1=g_eb[64 * hl:64 * hl + 64, dc, :], op0=ALU.mult, op1=ALU.mult),
                        reads=[("ps", b), ("g_eb", dc)], writes=[("g_qt", 2 * dc + hl)])
            sl, wk = wload([(0, 8, 256, win_ap(l, C_KB, 256))])
            W = wview(sl, 8, 256)
            for dc in range(2):
                b = psum(MMB)
                mm_group(ps[b][:], [(W[:, kc, dc * 128:(dc + 1) * 128], hT[:, kc, tb_sl(tb)]) for kc in range(8)],
                         reads=hr + [wk], writes=[("ps", b)])
                S.add("dve", lambda e, b=b, dc=dc: e.tensor_tensor(out=g_kt[:, dc, :], in0=ps[b][:], in1=g_enb[:, dc, :], op=ALU.mult),
                      reads=[("ps", b), ("g_enb", dc)], writes=[("g_kt", dc)])
            for tt in range(4):
                b = psum(MMB)
                t0 = tb * 512 + tt * 128
                mm_group(ps[b][:, 0:256], [(hT[:, kc, t0:t0 + 128], W[:, kc, :]) for kc in range(8)],
                         reads=hr + [wk], writes=[("ps", b)])
                for p in range(2):
                    S.add("dve", lambda e, b=b, tt=tt, p=p: e.tensor_tensor(
                        out=g_kend[64 * p:64 * p + 64, tt, p, :], in0=ps[b][64 * p:64 * p + 64, 0:256], in1=g_eend[64 * p:64 * p + 64, tt, :], op=ALU.mult),
                        reads=[("ps", b), ("g_eend", tt)], writes=[("g_kend", tt, p)])
            OB = [4, 5, 6, 7]

            def intra(tt):
                b = psum(MMB)
                for h in range(4):
                    dc = h // 2
                    S.add("pe", lambda e, b=b, dc=dc, tt=tt, h=h: e.matmul(
                        ps[b][:, h * 128:(h + 1) * 128], lhsT=g_kt[:, dc, tt * 128:(tt + 1) * 128],
                        rhs=g_qt[:, h, tt * 128:(tt + 1) * 128], start=True, stop=True),
                        reads=[("g_kt", dc), ("g_qt", h)], writes=[("ps", b)])
                for h in range(4):
                    att = g_att[tt * 4 + h]
                    S.add("dve", lambda e, b=b, att=att, h=h: e.tensor_tensor(out=att[:], in0=ps[b][:, h * 128:(h + 1) * 128], in1=maskT, op=ALU.mult),
                          reads=[("ps", b), "csts"], writes=[("g_att", tt * 4 + h)])

            for tt in range(INTRA_AHEAD):
                intra(tt)
            for cc in range(8):
                c = tb * 8 + cc
                tt, p = cc // 2, cc % 2
                if p == 1 and tt + INTRA_AHEAD < 4:
                    intra(tt + INTRA_AHEAD)
                for dc in range(2):
                    src, dst = c % 2, (c + 1) % 2
                    S.add("act", lambda e, dc=dc, cc=cc, src=src: e.activation(out=g_S2b[:, dc, cc, :], in_=g_S2[:, dc, src, :], func=AF.Copy),
                          reads=[("S2", dc, src)], writes=[("S2b", dc, cc)])
                    b = psum(MMB)
                    S.add("pe", lambda e, b=b, dc=dc, p=p, tt=tt: e.matmul(
                        ps[b][:, 0:256], lhsT=g_kend[:, tt, p, dc * 128:(dc + 1) * 128],
                        rhs=g_vb[:, tt, dc * 256:(dc + 1) * 256], start=True, stop=True),
                        reads=[("g_kend", tt, p), ("g_vb", tt, dc)], writes=[("ps", b)])
                    S.add("dve", lambda e, b=b, dc=dc, c=c, src=src, dst=dst: e.scalar_tensor_tensor(
                        out=g_S2[:, dc, dst, :], in0=g_S2[:, dc, src, :], scalar=g_dec[:, dc, c:c + 1], in1=ps[b][:, 0:256],
                        op0=ALU.mult, op1=ALU.add),
                        reads=[("ps", b), ("S2", dc, src), ("g_dec", dc)], writes=[("S2", dc, dst)])
                o_tts = []
                if cc % 2 == 1 and cc >= 3:
                    o_tts.append((cc - 3) // 2)
                if cc == 7:
                    o_tts.append(3)
                for tt in o_tts:
                    for h in range(4):
                        dc, hl = h // 2, h % 2
                        pb = 64 * hl
                        ob = OB[h]
                        att = g_att[tt * 4 + h]

                        def ofn(e, ob=ob, h=h, dc=dc, hl=hl, tt=tt, att=att):
                            e.matmul(ps[ob][:, tt * 128:(tt + 1) * 128], lhsT=g_vb[:, tt, h * 128:(h + 1) * 128], rhs=att[:],
                                     start=True, stop=False, skip_group_check=True)
                            ins = None
                            for pp in range(2):
                                cs_ = tt * 128 + 64 * pp
                                ins = e.matmul(ps[ob][:, cs_:cs_ + 64],
                                               lhsT=g_S2b[:, dc, tt * 2 + pp, hl * 128:(hl + 1) * 128],
                                               rhs=g_qt[:, h, cs_:cs_ + 64],
                                               start=False, stop=(pp == 1), skip_group_check=True)
                            return ins

                        S.add("pe", ofn, reads=[("g_att", tt * 4 + h), ("g_vb", tt, h // 2), ("g_qt", h),
                                                ("S2b", dc, tt * 2), ("S2b", dc, tt * 2 + 1)],
                              writes=[("ps", ob)])
            for hp2 in range(2):
                hs = (2 * hp2, 2 * hp2 + 1)
                nb_ = {}
                for h in hs:
                    S.add("act", lambda e, ob=OB[h], sq=g_sq[h % 2]: e.activation(out=sq[:], in_=ps[ob][:], func=AF.Square),
                          reads=[("ps", OB[h])], writes=[("g_sq", h % 2)])
                for h in hs:
                    b = psum(MMB)
                    nb_[h] = b
                    S.add("pe", lambda e, b=b, sq=g_sq[h % 2]: e.matmul(ps[b][:], lhsT=ones_hd, rhs=sq[:], start=True, stop=True),
                          reads=[("g_sq", h % 2), "csts_b"], writes=[("ps", b)])
                for h in hs:
                    S.add("act", lambda e, b=nb_[h], rs=g_rs[h % 2]: e.activation(out=rs[:], in_=ps[b][:], func=AF.Ln, bias=EPS),
                          reads=[("ps", nb_[h])], writes=[("g_rs", h % 2)])
                for h in hs:
                    S.add("act", lambda e, rs=g_rs[h % 2]: e.activation(out=rs[:], in_=rs[:], func=AF.Exp, scale=-0.5),
                          reads=[("g_rs", h % 2)], writes=[("g_rs", h % 2)])
                for h in hs:
                    S.add("dve", lambda e, rs=g_rs[h % 2], wgt=g_wgt[h % 2], h=h: e.scalar_tensor_tensor(
                        out=wgt[:], in0=rs[:], scalar=0.5, in1=g_srb[:, h, :], op0=ALU.mult, op1=ALU.mult),
                        reads=[("g_rs", h % 2), ("g_srb", h)], writes=[("g_wgt", h % 2)])
                for h in hs:
                    S.add("dve", lambda e, ob=OB[h], wgt=g_wgt[h % 2], h=h, tb=tb: e.scalar_tensor_tensor(
                        out=y_bT[:, h, tb_sl(tb)], in0=ps[ob][:], scalar=vecs[:, V_GLA + 4 * l + h:V_GLA + 4 * l + h + 1], in1=wgt[:],
                        op0=ALU.mult, op1=ALU.mult),
                        reads=[("ps", OB[h]), ("g_wgt", h % 2), "vecs"], writes=[("yb", h, tb)])

    def att_phase(l):
        S.add("sp", lambda e: e.dma_start(out=a_mh[:], in_=mh_d[l, :, :].rearrange("p (h x) -> p h x", x=256)), writes=["a_mh"], dma=True)
        for i in range(2):
            S.add("dve", lambda e, i=i: e.memset(a_v[i][:, :, 0, 64:128], 1.0), writes=[(("a_v", i), g4, 0) for g4 in range(4)])
            S.add("dve", lambda e, i=i: e.memset(a_v[i][:, :, 1, 0:64], 1.0), writes=[(("a_v", i), g4, 1) for g4 in range(4)])
            S.add("dve", lambda e, i=i: e.memset(a_k[i][64:128, 0, :], 0.0), writes=[(("a_k", i), tb, 0) for tb in range(4)])
            S.add("dve", lambda e, i=i: e.memset(a_k[i][0:64, 1, :], 0.0), writes=[(("a_k", i), tb, 1) for tb in range(4)])
        MMB = [0, 1, 2]
        SB_ = [0, 1, 2, 3, 4]
        OBK = [5, 6, 7]
        def proj_tasks(hp):
            q, k, v = a_q[hp % 2], a_k[hp % 2], a_v[hp % 2]
            qk_, kk_, vk_ = ("a_q", hp % 2), ("a_k", hp % 2), ("a_v", hp % 2)
            st = {}
            tasks = []

            def load_qk():
                sl, wk = wload([(0, 8, 128, win_ap(l, C_QA + hp * 128, 128)), (1024, 8, 128, win_ap(l, C_KA + hp * 128, 128))])
                st["Wq"], st["Wk"], st["wk_qk"] = wview(sl, 8, 128, 0), wview(sl, 8, 128, 1024), wk
                sl, wk = wload([(0, 8, 128, win_ap(l, C_VA + hp * 128, 128))])
                st["Wv"], st["wk_v"] = wview(sl, 8, 128, 0), wk

            def qk_task(which, dst, dk, tb):
                def f():
                    if "Wq" not in st:
                        load_qk()
                    Wx = st[which]
                    b = psum(MMB)
                    mm_group(ps[b][:], [(Wx[:, kc, :], hT[:, kc, tb_sl(tb)]) for kc in range(8)],
                             reads=h_reads(tb) + [st["wk_qk"]], writes=[("ps", b)])
                    if which == "Wq":
                        S.add("dve", lambda e: e.tensor_copy(out=dst[:, tb_sl(tb)], in_=ps[b][:]),
                              reads=[("ps", b)], writes=[(dk, tb)])
                    else:
                        S.add("dve", lambda e: e.tensor_copy(out=dst[0:64, 0, tb_sl(tb)], in_=ps[b][0:64, :]),
                              reads=[("ps", b)], writes=[(dk, tb, 0)])
                        S.add("dve", lambda e: e.tensor_copy(out=dst[64:128, 1, tb_sl(tb)], in_=ps[b][64:128, :]),
                              reads=[("ps", b)], writes=[(dk, tb, 1)])
                return f

            def v_task(g4):
                def f():
                    Wv = st["Wv"]
                    b = psum(MMB)
                    for j in range(4):
                        gt = g4 * 4 + j
                        mm_group(ps[b][:, j * 128:(j + 1) * 128], [(hT[:, kc, gt * 128:(gt + 1) * 128], Wv[:, kc, :]) for kc in range(8)],
                                 reads=h_reads(g4) + [st["wk_v"]], writes=[("ps", b)])
                    pv = ps[b][:].rearrange("p (j c) -> p j c", c=128)
                    S.add("dve", lambda e: e.tensor_copy(out=v[:, g4 * 4:(g4 + 1) * 4, 0, 0:64], in_=pv[:, :, 0:64]),
                          reads=[("ps", b)], writes=[(vk_, g4, 0)])
                    S.add("dve", lambda e: e.tensor_copy(out=v[:, g4 * 4:(g4 + 1) * 4, 1, 64:128], in_=pv[:, :, 64:128]),
                          reads=[("ps", b)], writes=[(vk_, g4, 1)])
                return f

            for tb in range(4):
                tasks.append(qk_task("Wq", q, qk_, tb))
            for tb in range(4):
                tasks.append(qk_task("Wk", k, kk_, tb))
            for g4 in range(4):
                tasks.append(v_task(g4))
            tasks.append(load_qk)
            return tasks

        next_tasks = proj_tasks(0)
        next_tasks.pop()()
        for hp in range(4):
            q, k, v = a_q[hp % 2], a_k[hp % 2], a_v[hp % 2]
            qk_, kk_, vk_ = ("a_q", hp % 2), ("a_k", hp % 2), ("a_v", hp % 2)
            for f in next_tasks:
                f()
            next_tasks = proj_tasks(hp + 1) if hp + 1 < 4 else []
            if next_tasks:
                next_tasks.pop()()
            tiles = []
            for qb in range(4):
                c0 = 8 * qb
                for hd in range(2):
                    unit = []
                    for m in (4, 5, 6, 7, 3, 2, 1, 0):
                        j0 = c0 - 8 + 2 * m
                        if j0 < 0:
                            continue
                        lo, hi = max(0, 2 * m - 8), min(7, 2 * m + 1)
                        if m >= 4:
                            nd_hi, x0 = min(2 * m - 5, 7), 0
                            ndc = (nd_hi - lo + 1) * 64
                        elif m == 3:
                            ndc, x0 = 128, 128
                        else:
                            ndc, x0 = 0, 0
                        unit.append(dict(qb=qb, hd=hd, m=m, kt=j0 // 2, ncol=(hi - lo + 1) * 64, q0=(c0 + lo) * 64,
                                         oc=lo * 64, ndc=ndc, x0=x0, first=False, last=False))
                    unit[0]["first"] = True
                    unit[-1]["last"] = True
                    tiles.extend(unit)

            def emit_S(t):
                sb = psum(SB_)
                t["sb"] = sb
                pb = 64 * t["hd"]
                kt, q0, ncol = t["kt"], t["q0"], t["ncol"]
                hd_ = t["hd"]
                S.add("pe", lambda e, k=k, q=q: e.matmul(ps[sb][:, 0:ncol], lhsT=k[:, hd_, kt * 128:(kt + 1) * 128],
                                                         rhs=q[:, q0:q0 + ncol], start=True, stop=True),
                      reads=[(kk_, kt // 4, hd_), (qk_, t["qb"])], writes=[("ps", sb)])

            def emit_P(t):
                sb, ncol, ndc, x0 = t["sb"], t["ncol"], t["ndc"], t["x0"]
                h = 2 * hp + t["hd"]
                pi = nxt("a_pT", 4)
                pT = a_pT[pi]
                pk = ("a_pT", pi)
                t["pT"], t["pk"] = pT, pk
                if ndc > 0:
                    ti = nxt("a_t", 3)
                    tt_ = a_t[ti]
                    tk = ("a_t", ti)
                    S.add("dve", lambda e: e.scalar_tensor_tensor(
                        out=tt_[:, 0:ndc], in0=ps[sb][:, 0:ndc], scalar=0.125, in1=a_mh[:, h, x0:x0 + ndc],
                        op0=ALU.mult, op1=ALU.add),
                        reads=[("ps", sb), "a_mh"], writes=[tk, ("psx", sb)])
                    S.add("act", lambda e: e.activation(out=pT[:, 0:ndc], in_=tt_[:, 0:ndc], func=AF.Exp),
                          reads=[tk], writes=[pk])
                if ndc < ncol:
                    S.add("act", lambda e: e.activation(
                        out=pT[:, ndc:ncol], in_=ps[sb][:, ndc:ncol], func=AF.Exp, scale=0.125,
                        bias=vecs[:, V_CB + 8 * l + h:V_CB + 8 * l + h + 1]),
                        reads=[("ps", sb), "vecs", ("psx", sb)], writes=[pk])

            cur_ob = [None]

            def emit_PV(t):
                if t["first"]:
                    cur_ob[0] = psum(OBK)
                ob = cur_ob[0]
                t["ob"] = ob
                ncol, m, hd, kt, oc, pT = t["ncol"], t["m"], t["hd"], t["kt"], t["oc"], t["pT"]
                if m >= 4:
                    zr = (64, 128, 0, 64)
                else:
                    zr = (0, 64, ncol - 64, ncol)
                pk = t["pk"]
                S.add("pool", lambda e: e.memset(pT[zr[0]:zr[1], zr[2]:zr[3]], 0.0), writes=[pk])
                isfirst = t["first"]
                S.add("pe", lambda e, v=v: e.matmul(ps[ob][:, oc:oc + ncol], lhsT=v[:, kt, hd, :], rhs=pT[:, 0:ncol],
                                                    start=isfirst, stop=False, skip_group_check=True),
                      reads=[pk, (vk_, kt // 4, hd)], writes=[("ps", ob)])

            def emit_norm_act(t):
                ob, hd = t["ob"], t["hd"]
                pb = 64 * hd
                so = 64 - pb
                rci = nxt("a_rc", 2)
                rc = a_rc[rci]
                rk = ("a_rc", rci)
                t["rc"], t["rk"] = rc, rk
                S.add("act", lambda e: e.activation(out=rc[pb:pb + 64, :], in_=ps[ob][so:so + 64, :], func=AF.Ln),
                      reads=[("ps", ob)], writes=[rk])
                S.add("act", lambda e: e.activation(out=rc[pb:pb + 64, :], in_=rc[pb:pb + 64, :], func=AF.Exp, scale=-1.0),
                      reads=[rk], writes=[rk, ("psx", ob)])

            def emit_norm_dve(t):
                ob, hd, qb, rc, rk = t["ob"], t["hd"], t["qb"], t["rc"], t["rk"]
                pb = 64 * hd
                S.add("dve", lambda e, hp=hp: e.tensor_tensor(
                    out=y_aT[pb:pb + 64, hp, tb_sl(qb)], in0=ps[ob][pb:pb + 64, :], in1=rc[pb:pb + 64, :], op=ALU.mult),
                    reads=[("ps", ob), rk, ("psx", ob)], writes=[("ya", hp, qb, hd)])

            LA = 4
            D_ACT, D_DVE = 2, 6
            nt = len(tiles)
            pending = []
            for i in range(min(LA, nt)):
                emit_S(tiles[i])
            for i in range(nt):
                if i + LA < nt:
                    emit_S(tiles[i + LA])
                emit_P(tiles[i])
                emit_PV(tiles[i])
                pending = [(c + 1, t) for (c, t) in pending]
                for (c, t) in pending:
                    if c == D_ACT:
                        emit_norm_act(t)
                    if c == D_DVE:
                        emit_norm_dve(t)
                pending = [(c, t) for (c, t) in pending if c < D_DVE]
                if tiles[i]["last"]:
                    pending.append((0, tiles[i]))
            for (c, t) in pending:
                if c < D_ACT:
                    emit_norm_act(t)
                emit_norm_dve(t)

    def merge_phase(l, fuse_n2=True):
        S.add("dve", lambda e: e.memset(m_t[0][:, 0:8], 0.0), writes=["mrg_gate", ("m_t", 0)])
        wx_mode[0] = True
        for fcp in range(4):
            slb, wkb = wload([(0, 4, 256, w_branch[l, 0, :, fcp * 256:(fcp + 1) * 256].rearrange("(k p) c -> p k c", p=128)),
                              (1024, 4, 256, w_branch[l, 1, :, fcp * 256:(fcp + 1) * 256].rearrange("(k p) c -> p k c", p=128))])
            Wb = wview(slb, 8, 256)
            sla, wka = wload([(0, 8, 256, win_ap(l, C_GA + fcp * 256, 256))])
            Wa = wview(sla, 8, 256)
            slg, wkg = wload([(0, 8, 256, win_ap(l, C_GB + fcp * 256, 256))])
            Wg = wview(slg, 8, 256)
            for fi in range(2):
                fc = 2 * fcp + fi
                cs = slice(fi * 128, (fi + 1) * 128)
                for tb in range(4):
                    hr = h_reads(tb)
                    bga, bgb, bua, bub = psum(ALLB), psum(ALLB), psum(ALLB), psum(ALLB)
                    mm_group(ps[bga][:], [(Wa[:, kc, cs], hT[:, kc, tb_sl(tb)]) for kc in range(8)], reads=hr + [wka], writes=[("ps", bga)])
                    mm_group(ps[bgb][:], [(Wg[:, kc, cs], hT[:, kc, tb_sl(tb)]) for kc in range(8)], reads=hr + [wkg], writes=[("ps", bgb)])
                    mm_group(ps[bua][:], [(Wb[:, kc, cs], y_aT[:, kc, tb_sl(tb)]) for kc in range(4)],
                             reads=[("ya", kc, tb, hd) for kc in range(4) for hd in range(2)] + [wkb], writes=[("ps", bua)])
                    mm_group(ps[bub][:], [(Wb[:, 4 + kc, cs], y_bT[:, kc, tb_sl(tb)]) for kc in range(4)],
                             reads=[("yb", kc, tb) for kc in range(4)] + [wkb], writes=[("ps", bub)])
                    i0, i1 = nxt("m_th", 4), nxt("m_th", 4)
                    S.add("act", lambda e, b=bga, i0=i0: e.activation(out=m_th[i0][:], in_=ps[b][:], func=AF.Tanh, scale=0.5),
                          reads=[("ps", bga)], writes=[("m_th", i0)])
                    S.add("act", lambda e, b=bgb, i1=i1: e.activation(out=m_th[i1][:], in_=ps[b][:], func=AF.Tanh, scale=0.5),
                          reads=[("ps", bgb)], writes=[("m_th", i1)])
                    j0, j1 = nxt("m_t", 4), nxt("m_t", 4)
                    S.add("dve", lambda e, b=bua, i0=i0, j0=j0: e.scalar_tensor_tensor(out=m_t[j0][:], in0=m_th[i0][:], scalar=1.0, in1=ps[b][:],
                                                                                      op0=ALU.add, op1=ALU.mult),
                          reads=[("ps", bua), ("m_th", i0)], writes=[("m_t", j0)])
                    S.add("dve", lambda e, b=bub, i1=i1, j1=j1: e.scalar_tensor_tensor(out=m_t[j1][:], in0=m_th[i1][:], scalar=1.0, in1=ps[b][:],
                                                                                      op0=ALU.add, op1=ALU.mult),
                          reads=[("ps", bub), ("m_th", i1)], writes=[("m_t", j1)])
                    S.add("dve", lambda e, j0=j0, j1=j1, fc=fc, tb=tb: e.tensor_tensor(out=m_mg[:, fc, tb_sl(tb)], in0=m_t[j0][:], in1=m_t[j1][:], op=ALU.add),
                          reads=[("m_t", j0), ("m_t", j1)], writes=[("mg", fc, tb)])
        Wo = []
        for fcp in range(4):
            sl, wk = wload([(0, 8, 256, w_out[l, :, fcp * 256:(fcp + 1) * 256].rearrange("(k p) c -> p k c", p=128))])
            Wo.append((wview(sl, 8, 256), wk))

        def outproj(tb):
            for fc in range(8):
                W, wk = Wo[fc // 2]
                fi = fc % 2
                b = psum(ALLB)
                mm_group(ps[b][:], [(W[:, kc, fi * 128:(fi + 1) * 128], m_mg[:, kc, tb_sl(tb)]) for kc in range(8)],
                         reads=[("mg", kc, tb) for kc in range(8)] + [wk], writes=[("ps", b)])
                S.add("dve", lambda e, b=b, fc=fc, tb=tb: e.scalar_tensor_tensor(
                    out=resT[:, fc, tb_sl(tb)], in0=ps[b][:], scalar=0.5, in1=resT[:, fc, tb_sl(tb)], op0=ALU.mult, op1=ALU.add),
                    reads=[("ps", b), ("res", fc, tb)], writes=[("res", fc, tb)])

        n2t = (n2_sq, n2_ln, "n2")
        outproj(0)
        for tb in range(1, 4):
            outproj(tb)
            if fuse_n2:
                rmsnorm(V_MLP + 8 * l, h_dst, h_keys, tbs=(tb - 1,), temps=n2t)
        if fuse_n2:
            rmsnorm(V_MLP + 8 * l, h_dst, h_keys, tbs=(3,), temps=n2t)
        wx_mode[0] = False

    def mlp_phase(l):
        statq = []

        def emit_stat(item):
            qi, fc, tb = item
            S.add("pe", lambda e: e.matmul(ps[4 + tb][:], lhsT=ones_mean, rhs=l_sq[qi][:], start=(fc == 0), stop=(fc == 7),
                                           skip_group_check=True),
                  reads=[("l_sq", qi), "csts_b"], writes=[("ps", 4 + tb)])

        for half in range(2):
            for j in range(8):
                sl, wk = wload([(0, 8, 256, w_up[l, :, half * 2048 + j * 256: half * 2048 + (j + 1) * 256].rearrange("(k p) c -> p k c", p=128))])
                W = wview(sl, 8, 256)
                for fi in range(2):
                    ffc = 2 * j + fi
                    for tb in range(4):
                        b = psum(ALLB)
                        mm_group(ps[b][:], [(W[:, kc, fi * 128:(fi + 1) * 128], hT[:, kc, tb_sl(tb)]) for kc in range(8)],
                                 reads=h_reads(tb) + [wk], writes=[("ps", b)])
                        ri = nxt("l_r", 3)
                        S.add("act", lambda e, b=b, ri=ri: e.activation(out=l_r[ri][:], in_=ps[b][:], func=AF.Relu),
                              reads=[("ps", b)], writes=[("l_r", ri)])
                        S.add("dve", lambda e, ri=ri, ffc=ffc, tb=tb: e.tensor_tensor(out=l_a[:, ffc, tb_sl(tb)], in0=l_r[ri][:], in1=l_r[ri][:], op=ALU.mult),
                              reads=[("l_r", ri)], writes=[("la", ffc, tb)])
            for fc in range(8):
                sl, wk = wload([(0, 16, 128, w_down[l, half * 2048:(half + 1) * 2048, fc * 128:(fc + 1) * 128].rearrange("(k p) c -> p k c", p=128))])
                W = wview(sl, 16, 128)
                for tb in range(4):
                    b = psum(ALLB if half == 0 else [0, 1, 2, 3])
                    mm_group(ps[b][:], [(W[:, kc, :], l_a[:, kc, tb_sl(tb)]) for kc in range(16)],
                             reads=[("la", kc, tb) for kc in range(16)] + [wk], writes=[("ps", b)])
                    S.add("dve", lambda e, b=b, fc=fc, tb=tb: e.tensor_tensor(out=resT[:, fc, tb_sl(tb)], in0=ps[b][:], in1=resT[:, fc, tb_sl(tb)], op=ALU.add),
                          reads=[("ps", b), ("res", fc, tb)], writes=[("res", fc, tb)])
                    if half == 1:
                        qi = nxt("l_sq", 6)
                        S.add("act", lambda e, qi=qi, fc=fc, tb=tb: e.activation(out=l_sq[qi][:], in_=resT[:, fc, tb_sl(tb)], func=AF.Square),
                              reads=[("res", fc, tb)], writes=[("l_sq", qi)])
                        statq.append((qi, fc, tb))
                        if len(statq) > 4:
                            emit_stat(statq.pop(0))
            if half == 1:
                while statq:
                    emit_stat(statq.pop(0))


    out_ops = []

    def dump(src, nchunks, is_bf16):
        S.barrier(include_pool=True)
        for c in range(nchunks):
            out_ops.append(S.add("pool", lambda e, c=c: e.dma_start(out=outT[c * 128:(c + 1) * 128, :], in_=src[:, c, :]), dma=True))

    done = False
    for l in range(L):
        S.epoch = l
        if l == 0:
            rmsnorm(V_MIX + 8 * l, h_dst, h_keys)
        else:
            norm_tail(V_MIX + 8 * l, h_out)
        if tap == ("n1", l):
            dump(hT, 8, True); done = True; break
        S.barrier()
        gla_phase(l)
        if tap == ("gla", l):
            dump(y_bT, 4, True); done = True; break
        S.barrier()
        att_phase(l)
        if tap == ("att", l):
            dump(y_aT, 4, True); done = True; break
        S.barrier()
        merge_phase(l)
        if tap == ("mix", l):
            dump(resT, 8, False); done = True; break
        mlp_phase(l)
        if tap == ("layer", l):
            dump(resT, 8, False); done = True; break
    if not done:
        S.barrier()
        if final_norm:
            def f_out(c, tb, ln, lnk, gcol):
                fi = nxt("f_o", 4)
                S.add("dve", lambda e: e.scalar_tensor_tensor(
                    out=f_o[fi][:], in0=resT[:, c, tb_sl(tb)], scalar=vecs[:, gcol + c:gcol + c + 1], in1=ln[:],
                    op0=ALU.mult, op1=ALU.mult),
                    reads=[("res", c, tb), lnk, "vecs"], writes=[("f_o", fi)])
                out_ops.append(S.add("sp", lambda e: e.dma_start(out=outT[c * 128:(c + 1) * 128, tb_sl(tb)], in_=f_o[fi][:]),
                                     reads=[("f_o", fi)], dma=True))

            norm_tail(V_FIN, f_out)
        else:
            for c in range(8):
                out_ops.append(S.add("sp", lambda e, c=c: e.dma_start(out=outT[c * 128:(c + 1) * 128, :], in_=resT[:, c, :]),
                                     reads=[("res", c, tb) for tb in range(4)], dma=True))
    S.emit(final_wait_ops=out_ops)
    return nc


def _consts():
    t = np.arange(128)
    same = (t[:, None] // 64) == (t[None, :] // 64)
    U = (same & (t[:, None] <= t[None, :])).astype(np.float32)
    R = (same & (t[:, None] > t[None, :])).astype(np.float32)
    M = (same & (t[None, :] >= t[:, None])).astype(np.float32)
    c = np.zeros((128, 5 * 128), np.float32)
    c[:, 0:128] = 1.0 / 1024
    c[:, 128:256] = 1.0 / 128
    c[:, 256:384] = U
    c[:, 384:512] = R
    c[:, 512:640] = M
    return c


def _prep_shared(inp, layers):
    L = len(layers)
    f32 = lambda a: np.ascontiguousarray(np.asarray(a, dtype=np.float32))
    sh = {}
    sh["w_in"] = f32(inp["w_in"][layers])
    sh["w_branch"] = f32(inp["w_branch"][layers])
    sh["w_out"] = f32(inp["w_out"][layers])
    sh["w_up"] = f32(inp["w_up"][layers])
    sh["w_down"] = f32(inp["w_down"][layers])
    sh["wg"] = f32(inp["w_gate_lr"][layers])
    sh["bg"] = f32(np.broadcast_to(np.asarray(inp["b_gate"])[layers][:, None, :], (L, 128, 256)))
    rb = np.asarray(inp["rel_bias"], dtype=np.float32)[layers]
    kk = np.arange(128)[:, None]
    xx = np.arange(256)[None, :]
    idx = np.clip(xx - kk, -128, 128) + 128
    mh = rb[:, :, idx]
    sh["mh"] = f32(np.transpose(mh, (0, 2, 1, 3)).reshape(L, 128, 8 * 256))
    vec = np.zeros((128, 128), np.float32)
    mg = np.asarray(inp["mix_norm_g"], dtype=np.float32)[layers]
    lg = np.asarray(inp["mlp_norm_g"], dtype=np.float32)[layers]
    gg = np.asarray(inp["gla_norm_g"], dtype=np.float32)[layers]
    for i in range(L):
        vec[:, 8 * i:8 * i + 8] = mg[i].reshape(8, 128).T
        vec[:, 32 + 8 * i:32 + 8 * i + 8] = lg[i].reshape(8, 128).T
        vec[:, 72 + 4 * i:72 + 4 * i + 4] = gg[i].reshape(4, 128).T
        vec[:, 88 + 8 * i:88 + 8 * i + 8] = rb[i, :, 256][None, :]
    vec[:, 64:72] = np.asarray(inp["final_norm_g"], dtype=np.float32).reshape(8, 128).T
    sh["vecs"] = vec
    sh["csts"] = _consts()
    return sh


_PROGS = {}


def _prog(key, **kw):
    if key not in _PROGS:
        _PROGS[key] = build(**kw)
    return _PROGS[key]


def kernel(**inp):
    x = np.asarray(inp["x"], dtype=np.float32)
    xTs = [np.ascontiguousarray(x[b].T) for b in range(NCORES)]
    sh = _prep_shared(inp, list(range(DEPTH)))
    nc = _prog("full", n_layers=DEPTH)
    in_maps = [dict(sh, xT=xTs[b]) for b in range(NCORES)]
    res = run_bass_kernel_spmd(nc, in_maps, core_ids=list(range(NCORES)))
    out = np.stack([np.ascontiguousarray(res.results[b]["outT"].T) for b in range(NCORES)], axis=0)
    return out.astype(np.float32)
```
